# Optimizing a Trainium2 kernel written in Bass

```python
import math
import jax, jax.numpy as jnp
from jax import lax
import numpy as np

D_MODEL = 1024
BATCH = 32
SEQ = 256
DEPTH = 2
DEC_BATCH = 2
DEC_SEQ = 2048
PAST_LEN = 512

GRID_W = 64
HEAD_DIM = 64
POOL_WIDTH = D_MODEL // 4
POOL_GROUPS = 4
POOL_GROUP_W = POOL_WIDTH // POOL_GROUPS
POOL_WINDOWS = (2, 4, 8, 16)
DIFF_HEADS = (D_MODEL // 2) // (2 * HEAD_DIM)
DIFF_QK_W = DIFF_HEADS * 2 * HEAD_DIM
DIFF_V_W = DIFF_HEADS * 2 * HEAD_DIM
NA_HEADS = (D_MODEL // 4) // HEAD_DIM
NA_W = NA_HEADS * HEAD_DIM
NA_ROWS = 8
NA_COLS = 16
N_BRANCH = 3
SPLIT_SIZES = (POOL_WIDTH, DIFF_QK_W, DIFF_QK_W, DIFF_V_W, NA_W, NA_W, NA_W, N_BRANCH * D_MODEL)
D_IN = POOL_WIDTH + 2 * DIFF_QK_W + DIFF_V_W + 3 * NA_W + N_BRANCH * D_MODEL
D_FF = ((8 * D_MODEL // 3 + 127) // 128) * 128
N_MOD = 9
QB = 128
ROPE_BASE = 10000.0
LN_EPS = 1e-5
RMS_EPS = 1e-5
ALPHA = (2 * DEPTH) ** 0.25
BETA = (8 * DEPTH) ** -0.25
ATTN_SCALE = HEAD_DIM ** -0.5
NEG_INF = -1e30

kernel_name = "hybrid_diffusion_pool_diffattn_natten_step"


def layer_norm(x, g, b):
    x32 = x.astype(jnp.float32)
    mu = jnp.mean(x32, axis=-1, keepdims=True)
    var = jnp.mean(jnp.square(x32 - mu), axis=-1, keepdims=True)
    return ((x32 - mu) * lax.rsqrt(var + LN_EPS) * g + b).astype(x.dtype)


def swiglu(h, w1, w3, w2):
    return (jax.nn.silu(h @ w1) * (h @ w3)) @ w2


def split_in(z):
    outs = []
    o = 0
    for s in SPLIT_SIZES:
        outs.append(z[..., o:o + s])
        o += s
    return outs


def rope2d(x):
    n = x.shape[1]
    t = jnp.arange(n)
    row = (t // GRID_W).astype(jnp.float32)
    col = (t % GRID_W).astype(jnp.float32)
    nf = HEAD_DIM // 4
    inv = ROPE_BASE ** (-jnp.arange(nf, dtype=jnp.float32) / nf)

    def rot(xh, pos):
        ang = (pos[:, None] * inv)[:, None, None, :]
        cos, sin = jnp.cos(ang), jnp.sin(ang)
        x1, x2 = xh[..., :nf], xh[..., nf:]
        return jnp.concatenate([x1 * cos - x2 * sin, x1 * sin + x2 * cos], axis=-1)

    x32 = x.astype(jnp.float32)
    out = jnp.concatenate([rot(x32[..., :HEAD_DIM // 2], row), rot(x32[..., HEAD_DIM // 2:], col)], axis=-1)
    return out.astype(x.dtype)


def multiscale_pool(a, w_pool, scale):
    n = a.shape[-2]
    af = a.astype(jnp.float32)
    cs = jnp.concatenate([jnp.zeros_like(af[..., :1, :]), jnp.cumsum(af, axis=-2)], axis=-2)
    t = jnp.arange(n)
    outs = []
    for g, w in enumerate(POOL_WINDOWS):
        lo = jnp.clip(t - w // 2, 0, n)
        hi = jnp.clip(t + w // 2, 0, n)
        seg = cs[..., g * POOL_GROUP_W:(g + 1) * POOL_GROUP_W]
        mean = (jnp.take(seg, hi, axis=-2) - jnp.take(seg, lo, axis=-2)) / (hi - lo).astype(jnp.float32)[:, None]
        outs.append(mean - af[..., g * POOL_GROUP_W:(g + 1) * POOL_GROUP_W])
    p = jnp.stack(outs, axis=-2)
    y = jnp.einsum('...ngc,gcd->...ngd', p, w_pool)
    y = y.reshape(*y.shape[:-2], POOL_WIDTH) * scale
    return y.astype(a.dtype)


def diff_attention(q, k, v, lam, lam_init, g):
    b, n = q.shape[:2]
    qb = q.reshape(b, n // QB, QB, *q.shape[2:]).swapaxes(0, 1)

    def one_block(qblk):
        s = jnp.einsum('bqhmd,bkhmd->bhmqk', qblk, k).astype(jnp.float32) * ATTN_SCALE
        p = jax.nn.softmax(s, axis=-1)
        a = (p[:, :, 0] - lam * p[:, :, 1]).astype(v.dtype)
        return jnp.einsum('bhqk,bkhe->bqhe', a, v)

    o = lax.map(one_block, qb).swapaxes(0, 1).reshape(b, n, *v.shape[2:])
    o32 = o.astype(jnp.float32)
    o32 = o32 * lax.rsqrt(jnp.mean(jnp.square(o32), axis=-1, keepdims=True) + RMS_EPS) * g * (1.0 - lam_init)
    return o32.astype(v.dtype).reshape(b, n, -1)


def dense_attention(q, k, v):
    b, n, h, hd = q.shape
    qb = q.reshape(b, n // QB, QB, h, hd).swapaxes(0, 1)

    def one_block(qblk):
        s = jnp.einsum('bqhd,bkhd->bhqk', qblk, k).astype(jnp.float32) * ATTN_SCALE
        p = jax.nn.softmax(s, axis=-1).astype(v.dtype)
        return jnp.einsum('bhqk,bkhd->bqhd', p, v)

    return lax.map(one_block, qb).swapaxes(0, 1).reshape(b, n, h * hd)


def neighbourhood_attention(q, k, v, ck, cv, rpb):
    b, l, h, hd = q.shape
    rows = l // GRID_W
    kh = min(NA_ROWS, rows)
    qg = q.reshape(b, rows, GRID_W, h, hd)
    kg = k.reshape(b, rows, GRID_W, h, hd)
    vg = v.reshape(b, rows, GRID_W, h, hd)
    r = jnp.arange(rows)
    row_start = jnp.clip(r - kh // 2, 0, rows - kh)
    row_idx = row_start[:, None] + jnp.arange(kh)
    k_nb = kg[:, row_idx]
    v_nb = vg[:, row_idx]
    col = jnp.arange(GRID_W)
    col_start = jnp.clip(col - NA_COLS // 2, 0, GRID_W - NA_COLS)
    col_ok = (col[None, :] >= col_start[:, None]) & (col[None, :] < col_start[:, None] + NA_COLS)
    dr = row_idx - r[:, None] + (NA_ROWS - 1)
    dc = jnp.clip(col[None, :] - col[:, None], -(NA_COLS - 1), NA_COLS - 1) + (NA_COLS - 1)
    bias = rpb[:, dr][:, :, :, dc]
    bias = bias.transpose(0, 1, 3, 2, 4).astype(jnp.float32)
    s_nb = jnp.einsum('brwhd,brkvhd->bhrwkv', qg, k_nb).astype(jnp.float32) * ATTN_SCALE + bias[None]
    s_nb = jnp.where(col_ok[:, None, :], s_nb, NEG_INF)
    s_ctx = jnp.einsum('brwhd,bchd->bhrwc', qg, ck).astype(jnp.float32) * ATTN_SCALE
    s = jnp.concatenate([s_nb.reshape(b, h, rows, GRID_W, kh * GRID_W), s_ctx], axis=-1)
    p = jax.nn.softmax(s, axis=-1).astype(v.dtype)
    p_nb = p[..., :kh * GRID_W].reshape(b, h, rows, GRID_W, kh, GRID_W)
    p_ctx = p[..., kh * GRID_W:]
    o = jnp.einsum('bhrwkv,brkvhd->brwhd', p_nb, v_nb) + jnp.einsum('bhrwc,bchd->brwhd', p_ctx, cv)
    return o.reshape(b, l, h * hd)


def setup_inputs(seed: int = 0) -> dict:
    key = jax.random.key(seed)
    ks = jax.random.split(key, 32)
    f32 = jnp.float32

    def nrm(k, shape, scale=1.0):
        return jax.random.normal(k, shape, f32) * scale

    return {
        'x_prompt': nrm(ks[0], (BATCH, SEQ, D_MODEL)),
        'x_sample': nrm(ks[1], (DEC_BATCH, DEC_SEQ, D_MODEL)),
        'cache_diff_k': nrm(ks[2], (DEC_BATCH, DEPTH, PAST_LEN, DIFF_HEADS, 2, HEAD_DIM)),
        'cache_diff_v': nrm(ks[3], (DEC_BATCH, DEPTH, PAST_LEN, DIFF_HEADS, 2 * HEAD_DIM)),
        'cache_na_k': nrm(ks[4], (DEC_BATCH, DEPTH, PAST_LEN, NA_HEADS, HEAD_DIM)),
        'cache_na_v': nrm(ks[5], (DEC_BATCH, DEPTH, PAST_LEN, NA_HEADS, HEAD_DIM)),
        'c': nrm(ks[6], (DEC_BATCH, D_MODEL)),
        'c_ctx': nrm(ks[7], (D_MODEL,)),
        'w_mod': nrm(ks[8], (DEPTH, D_MODEL, N_MOD * D_MODEL), D_MODEL ** -0.5),
        'b_mod': nrm(ks[9], (DEPTH, N_MOD * D_MODEL), 0.02),
        'ln_g': 1.0 + nrm(ks[10], (DEPTH, 3, D_MODEL), 0.02),
        'ln_b': nrm(ks[11], (DEPTH, 3, D_MODEL), 0.02),
        'ffn1_w1': nrm(ks[12], (DEPTH, D_MODEL, D_FF), D_MODEL ** -0.5),
        'ffn1_w3': nrm(ks[13], (DEPTH, D_MODEL, D_FF), D_MODEL ** -0.5),
        'ffn1_w2': nrm(ks[14], (DEPTH, D_FF, D_MODEL), BETA * D_FF ** -0.5),
        'ffn2_w1': nrm(ks[15], (DEPTH, D_MODEL, D_FF), D_MODEL ** -0.5),
        'ffn2_w3': nrm(ks[16], (DEPTH, D_MODEL, D_FF), D_MODEL ** -0.5),
        'ffn2_w2': nrm(ks[17], (DEPTH, D_FF, D_MODEL), BETA * D_FF ** -0.5),
        'w_in': nrm(ks[18], (DEPTH, D_MODEL, D_IN), D_MODEL ** -0.5),
        'pool_w': nrm(ks[19], (DEPTH, POOL_GROUPS, POOL_GROUP_W, POOL_GROUP_W), POOL_GROUP_W ** -0.5),
        'pool_scale': 1.0 + nrm(ks[20], (DEPTH, POOL_WIDTH), 0.02),
        'w_pa': nrm(ks[21], (DEPTH, POOL_WIDTH, D_MODEL), POOL_WIDTH ** -0.5),
        'w_pb': nrm(ks[22], (DEPTH, DIFF_V_W, D_MODEL), DIFF_V_W ** -0.5),
        'w_pc': nrm(ks[23], (DEPTH, NA_W, D_MODEL), NA_W ** -0.5),
        'lam_q1': nrm(ks[24], (DEPTH, HEAD_DIM), 0.1),
        'lam_k1': nrm(ks[25], (DEPTH, HEAD_DIM), 0.1),
        'lam_q2': nrm(ks[26], (DEPTH, HEAD_DIM), 0.1),
        'lam_k2': nrm(ks[27], (DEPTH, HEAD_DIM), 0.1),
        'subln_g': 1.0 + nrm(ks[28], (DEPTH, 2 * HEAD_DIM), 0.02),
        'na_rpb': nrm(ks[29], (DEPTH, NA_HEADS, 2 * NA_ROWS - 1, 2 * NA_COLS - 1), 0.1),
        'w_out': nrm(ks[30], (DEPTH, D_MODEL, D_MODEL), BETA * D_MODEL ** -0.5),
    }


def reference(x_prompt, x_sample, cache_diff_k, cache_diff_v, cache_na_k, cache_na_v, c, c_ctx,
              w_mod, b_mod, ln_g, ln_b, ffn1_w1, ffn1_w3, ffn1_w2, ffn2_w1, ffn2_w3, ffn2_w2,
              w_in, pool_w, pool_scale, w_pa, w_pb, w_pc, lam_q1, lam_k1, lam_q2, lam_k2,
              subln_g, na_rpb, w_out):

    def modulation(cond, l):
        m = jax.nn.silu(cond) @ w_mod[l] + b_mod[l]
        return m.reshape(m.shape[0], 1, N_MOD, D_MODEL)

    def projections(h, l):
        b, n = h.shape[:2]
        a, qb, kb, vb, qc, kc, vc, gt = split_in(h @ w_in[l])
        return (a,
                qb.reshape(b, n, DIFF_HEADS, 2, HEAD_DIM),
                kb.reshape(b, n, DIFF_HEADS, 2, HEAD_DIM),
                vb.reshape(b, n, DIFF_HEADS, 2 * HEAD_DIM),
                qc.reshape(b, n, NA_HEADS, HEAD_DIM),
                kc.reshape(b, n, NA_HEADS, HEAD_DIM),
                vc.reshape(b, n, NA_HEADS, HEAD_DIM),
                gt)

    def diff_lambda(l):
        lam_init = 0.8 - 0.6 * math.exp(-0.3 * l)
        e1 = jnp.exp(jnp.sum(lam_q1[l].astype(jnp.float32) * lam_k1[l].astype(jnp.float32)))
        e2 = jnp.exp(jnp.sum(lam_q2[l].astype(jnp.float32) * lam_k2[l].astype(jnp.float32)))
        return e1 - e2 + lam_init, lam_init

    def merge(y_a, y_b, y_c, gt, l):
        g = jax.nn.sigmoid(gt.astype(jnp.float32)).astype(gt.dtype)
        g = g.reshape(*gt.shape[:-1], N_BRANCH, D_MODEL)
        mixed = (g[..., 0, :] * (y_a @ w_pa[l]) + g[..., 1, :] * (y_b @ w_pb[l])
                 + g[..., 2, :] * (y_c @ w_pc[l]))
        return mixed @ w_out[l]

    def mix_context(h, l):
        a, qb, kb, vb, qc, kc, vc, gt = projections(h, l)
        lam, lam_init = diff_lambda(l)
        y_a = multiscale_pool(a, pool_w[l], pool_scale[l])
        y_b = diff_attention(qb, kb, vb, lam, lam_init, subln_g[l])
        y_c = dense_attention(qc, kc, vc)
        return merge(y_a, y_b, y_c, gt, l), (kb, vb, kc, vc)

    def mix_latent(h, l):
        b, n = h.shape[:2]
        rows = n // GRID_W
        a, qb, kb, vb, qc, kc, vc, gt = projections(h, l)
        lam, lam_init = diff_lambda(l)
        y_a = multiscale_pool(a.reshape(b, rows, GRID_W, POOL_WIDTH), pool_w[l], pool_scale[l])
        y_a = y_a.reshape(b, n, POOL_WIDTH)
        k_all = jnp.concatenate([rope2d(kb), cache_diff_k[:, l]], axis=1)
        v_all = jnp.concatenate([vb, cache_diff_v[:, l]], axis=1)
        y_b = diff_attention(rope2d(qb), k_all, v_all, lam, lam_init, subln_g[l])
        y_c = neighbourhood_attention(qc, kc, vc, cache_na_k[:, l], cache_na_v[:, l], na_rpb[l])
        return merge(y_a, y_b, y_c, gt, l), None

    def block(x, m, l, mixer):
        def mod(z, i):
            return z * (1.0 + m[:, :, 3 * i + 1]) + m[:, :, 3 * i]

        h = swiglu(mod(x, 0), ffn1_w1[l], ffn1_w3[l], ffn1_w2[l])
        x = layer_norm(ALPHA * x + 0.5 * m[:, :, 2] * h, ln_g[l, 0], ln_b[l, 0])
        h, aux = mixer(mod(x, 1), l)
        x = layer_norm(ALPHA * x + m[:, :, 5] * h, ln_g[l, 1], ln_b[l, 1])
        h = swiglu(mod(x, 2), ffn2_w1[l], ffn2_w3[l], ffn2_w2[l])
        x = layer_norm(ALPHA * x + 0.5 * m[:, :, 8] * h, ln_g[l, 2], ln_b[l, 2])
        return x, aux

    x = x_prompt
    dk, dv, nk, nv = [], [], [], []
    for l in range(DEPTH):
        x, (kb, vb, kc, vc) = block(x, modulation(c_ctx[None, :], l), l, mix_context)
        dk.append(kb)
        dv.append(vb)
        nk.append(kc)
        nv.append(vc)
    y_prompt = x
    new_diff_k = jnp.stack(dk, axis=1)
    new_diff_v = jnp.stack(dv, axis=1)
    new_na_k = jnp.stack(nk, axis=1)
    new_na_v = jnp.stack(nv, axis=1)

    x = x_sample
    for l in range(DEPTH):
        x, _ = block(x, modulation(c, l), l, mix_latent)
    y_sample = x

    return (y_prompt, y_sample, new_diff_k, new_diff_v, new_na_k, new_na_v)
```

```python
import math
import numpy as np
import concourse.bass as bass
import concourse.mybir as mybir
from concourse.bass_utils import run_bass_kernel_spmd

F32 = mybir.dt.float32
BF16 = mybir.dt.bfloat16
AF = mybir.ActivationFunctionType
ALU = mybir.AluOpType

D = 1024
DFF = 2816
DIN = 5632
ALPHA = 4.0 ** 0.25
LN_EPS = 1e-5
RMS_EPS = 1e-5
NEG = -1e30
CELL = 1024


class Op:
    __slots__ = ("eng", "fn", "deps", "signal", "dma", "idx", "sigval")

    def __init__(self, eng, fn, dma):
        self.eng = eng
        self.fn = fn
        self.deps = []
        self.signal = False
        self.dma = dma
        self.idx = -1
        self.sigval = 0


class V:
    __slots__ = ("ap", "keys", "blk")

    def __init__(self, ap, keys):
        self.ap = ap
        self.keys = keys
        self.blk = None


class Prog:
    ENGS = ("pe", "act", "dve", "pool", "sp")

    def __init__(self):
        self.ops = {e: [] for e in self.ENGS}
        self.lastw = {}
        self.readers = {}
        self.dcount = {}
        self.tags = []
        self.phase = "init"

    def add(self, eng, fn, reads=(), writes=(), dma=None):
        if dma is not None:
            c = self.dcount.get(dma, 0) + 16
            self.dcount[dma] = c
            dma = (dma, c)
        op = Op(eng, fn, dma)
        deps = {}
        rk = []
        for r in reads:
            rk.extend(r.keys if isinstance(r, V) else [r])
        wk = []
        for w in writes:
            wk.extend(w.keys if isinstance(w, V) else [w])
        for k in rk:
            if isinstance(k, tuple) and k[0] == "ps" and k not in wk:
                wk.append(k)
        for k in rk:
            w = self.lastw.get(k)
            if w is not None:
                deps[id(w)] = (w, True)
        for k in wk:
            for rd in self.readers.get(k, ()):
                if id(rd) not in deps:
                    deps[id(rd)] = (rd, False)
            w = self.lastw.get(k)
            if w is not None and id(w) not in deps:
                deps[id(w)] = (w, False)
        for d, raw in deps.values():
            if d.dma is None and d.eng == eng:
                if eng == "pe":
                    continue
            if d.dma is None:
                d.signal = True
            op.deps.append(d)
        op.idx = len(self.ops[eng])
        self.ops[eng].append(op)
        if eng == "pe":
            self.tags.append(self.phase)
        for k in rk:
            self.readers.setdefault(k, []).append(op)
        for k in wk:
            self.lastw[k] = op
            self.readers[k] = []
        return op

    def emit(self, nc, block_engines, esem, dsem):
        for e in self.ENGS:
            c = 0
            for op in self.ops[e]:
                if op.dma is None and op.signal:
                    c += 1
                    op.sigval = c
        for e in self.ENGS:
            eng = block_engines[e]
            waited = {}
            for op in self.ops[e]:
                need = {}
                for d in op.deps:
                    if d.dma is not None:
                        key = ("d", d.dma[0])
                        val = d.dma[1]
                        sem = dsem[d.dma[0]]
                    else:
                        key = ("e", d.eng)
                        val = d.sigval
                        sem = esem[d.eng]
                    if waited.get(key, 0) >= val:
                        continue
                    if key not in need or need[key][1] < val:
                        need[key] = (sem, val)
                for key, (sem, val) in need.items():
                    eng.wait_ge(sem, val)
                    waited[key] = val
                if op.fn is None:
                    continue
                ins = op.fn(eng)
                if op.dma is not None:
                    ins.then_inc(dsem[op.dma[0]], 16)
                elif op.signal:
                    ins.then_inc(esem[e], 1)


class Builder:
    def __init__(self, nc):
        self.nc = nc
        self.P = Prog()
        self.dr = {}
        self.scr_i = 0
        self.bs_i = 0

    def din(self, name, shape):
        self.dr[name] = self.nc.dram_tensor(name, list(shape), F32, kind="ExternalInput").ap()
        return self.dr[name]

    def dout(self, name, shape):
        self.dr[name] = self.nc.dram_tensor(name, list(shape), F32, kind="ExternalOutput").ap()
        return self.dr[name]

    def view(self, tname, off, n, dims=None, pl=0, ph=128):
        t, esz = self.sb[tname]
        ap = t[pl:ph, off:off + n]
        if dims is not None:
            names = " ".join("d%d" % i for i in range(len(dims)))
            kw = {"d%d" % i: dims[i] for i in range(len(dims))}
            ap = ap.rearrange("p (%s) -> p %s" % (names, names), **kw)
        b0 = off * esz
        b1 = (off + n) * esz
        keys = [(tname, c) for c in range(b0 // CELL, (b1 - 1) // CELL + 1)]
        return V(ap, keys)

    def scratch(self, n=512):
        i = self.scr_i
        self.scr_i = (i + 1) % self.NSCR
        return self.view("SCR", i * 512, n)

    def bscratch(self, n=512):
        i = self.bs_i
        self.bs_i = (i + 1) % self.NBS
        return self.view("BS", i * 512, n)

    def mm(self, out, lhsT, rhs, start, stop, reads, writes):
        self.P.add("pe", lambda e: e.matmul(out, lhsT, rhs, start=start, stop=stop, skip_group_check=True),
                   reads=reads, writes=writes)
        self.note_reads(reads)

    def act(self, out, in_, func, reads, writes, bias=None, scale=None, accum_out=None):
        kw = {}
        if bias is not None:
            kw["bias"] = bias
        if scale is not None:
            kw["scale"] = scale
        if accum_out is not None:
            kw["accum_out"] = accum_out
        self.P.add("act", lambda e: e.activation(out, in_, func, **kw), reads=reads, writes=writes)

    def dve(self, fn, reads, writes):
        self.P.add("dve", fn, reads=reads, writes=writes)

    def dma(self, q, out, in_, semkey, reads, writes):
        self.P.add(q, lambda e: e.dma_start(out=out, in_=in_), reads=reads, writes=writes, dma=semkey)

    NSLOT = 4

    def wq_init(self, rec=None):
        self.wq_dry = rec is None
        self.wq_rec = [] if rec is None else rec[0]
        self.wq_last = [] if rec is None else rec[1]
        self.wq_n = 0
        self.opc = 0
        self.wq_next = 0
        self.wq_slotblk = [None] * self.NSLOT
        self.wq_view = {}

    def _wq_emit(self, b, slot):
        parts, shape = self.wq_rec[b]
        A, C = shape
        sv = self.view("WR", slot * 4096, A * C, dims=[A, C])
        sv.keys = [("w", slot)]
        sv.blk = b
        for (src, a0, c0) in parts:
            cols = src.shape[-1]
            s3 = src.rearrange("(a p) c -> p a c", p=128)
            na = s3.shape[1]
            dst = sv.ap[:, a0:a0 + na, c0:c0 + cols]
            self.dma("pool", dst, s3, ("w", slot), reads=[], writes=[sv])
        self.wq_slotblk[slot] = b
        self.wq_view[b] = sv

    def _wq_free_slot(self):
        for s_ in range(self.NSLOT):
            x = self.wq_slotblk[s_]
            if x is None or (x < self.wq_n and self.opc >= self.wq_last[x]):
                return s_
        return None

    def wq_pump(self):
        if self.wq_dry:
            return
        while self.wq_next < len(self.wq_rec):
            s_ = self._wq_free_slot()
            if s_ is None:
                return
            self._wq_emit(self.wq_next, s_)
            self.wq_next += 1

    def wload(self, parts, shape):
        norm = []
        c0 = 0
        for p_ in parts:
            if isinstance(p_, tuple):
                norm.append(p_)
            else:
                norm.append((p_, 0, c0))
                c0 += p_.shape[-1]
        b = self.wq_n
        self.wq_n += 1
        if self.wq_dry:
            self.wq_rec.append((norm, shape))
            self.wq_last.append(self.opc)
            A, C = shape
            sv = self.view("WR", 0, A * C, dims=[A, C])
            sv.keys = [("w", 0)]
            sv.blk = b
            return sv
        if b >= self.wq_next:
            assert b == self.wq_next
            s_ = self._wq_free_slot()
            assert s_ is not None, "weight ring full at block %d" % b
            self._wq_emit(b, s_)
            self.wq_next += 1
        self.wq_pump()
        return self.wq_view[b]

    def note_reads(self, reads):
        self.opc += 1
        for r in reads:
            if isinstance(r, V) and r.blk is not None:
                if self.wq_dry:
                    self.wq_last[r.blk] = self.opc
        self.wq_pump()


def build(nc):
    B = Builder(nc)
    P = B.P
    dr = B.dr
    B.din("xpT", [D, 1024]); B.din("xsT", [D, 2048])
    B.din("condT", [128, 16]); B.din("bmodT", [128, 2 * 72])
    B.din("lngT", [128, 48]); B.din("lnbT", [128, 48])
    B.din("w_mod", [2, D, 9216])
    for nme in ("ffn1_w1", "ffn1_w3", "ffn2_w1", "ffn2_w3"):
        B.din(nme, [2, D, DFF])
    B.din("ffn1_w2", [2, DFF, D]); B.din("ffn2_w2", [2, DFF, D])
    B.din("w_in", [2, D, DIN]); B.din("w_in_sw", [2, D, 1024])
    B.din("w_pa", [2, 256, D]); B.din("w_pb", [2, 512, D]); B.din("w_pc", [2, 256, D]); B.din("w_out", [2, D, D])
    B.din("poolbd", [2, 2, 128, 128]); B.din("poolsc", [128, 4])
    B.din("Pp", [128, 4 * 2 * 256]); B.din("Ps", [128, 4 * 128])
    B.din("lamv", [128, 2 * 4 * 64]); B.din("sgT", [128, 2])
    B.din("ropeC", [128, 2048]); B.din("ropeS", [128, 2048])
    B.din("nabias", [2, 4, 4, 8, 128, 512])
    B.din("cdkT", [2, 4, 128, 512]); B.din("cdv", [2, 512, 4, 128])
    B.din("cnkT", [2, 2, 128, 512]); B.din("cnv", [2, 512, 4, 64])
    B.din("onesb", [128, 384]); B.din("onesf", [128, 128]); B.din("wsel", [128, 4])
    B.dout("ypT", [D, 1024]); B.dout("ysT", [D, 512])
    B.dout("ndkT", [2, 512, 1024]); B.dout("ndv", [2, 1024, 512])
    B.dout("nnkT", [2, 256, 1024]); B.dout("nnv", [2, 1024, 256])

    return B


def emit_all(B, nc, sb, ps):
    P = B.P
    dr = B.dr
    B.sb = sb
    B.NSCR = 5
    B.NBS = 3
    PS = [V(ps[i], [("ps", i)]) for i in range(7)]
    PS7 = V(ps[7], [("ps", 7)])

    def psv(b, lo=0, n=512, pl=0, ph=128):
        return PS[b].ap[pl:ph, lo:lo + n]

    cst = {}

    def T(name):
        return sb[name][0]

    def cload(name, tname, n, q="sp"):
        v = B.view(tname, 0, n)
        B.dma(q, v.ap, dr[name], ("c", name), reads=[], writes=[v])
        cst[name] = v
        return v

    cload("condT", "condT", 16); cload("bmodT", "bmodT", 144); cload("lngT", "lngT", 48); cload("lnbT", "lnbT", 48)
    cload("poolsc", "poolsc", 4); cload("lamv", "LNS", 512); cload("sgT", "sgT", 2)
    cload("onesf", "onesf", 128)
    cload("wsel", "wsel", 4)
    cload("onesb", "onesb", 384, q="pool")
    cload("Pp", "Pp", 2048, q="pool"); cload("Ps", "Ps", 512, q="pool")
    bdv = B.view("poolbd", 0, 512, dims=[4, 128])
    B.dma("pool", bdv.ap, dr["poolbd"].rearrange("l c p d -> p (l c) d"), ("c", "poolbd"), reads=[], writes=[bdv])
    ones = cst["onesf"]
    epsv = B.view("EPS", 0, 2)
    B.dve(lambda e: e.memset(T("EPS")[:, 0:1], LN_EPS), reads=[], writes=[epsv])
    B.dve(lambda e: e.memset(T("EPS")[:, 1:2], RMS_EPS), reads=[], writes=[epsv])


    scv = B.view("scb", 0, 16, dims=[8, 2])
    B.act(scv.ap, B.view("condT", 0, 16, dims=[8, 2]).ap, AF.Silu, reads=[cst["condT"]], writes=[scv])

    MTv = B.view("MT", 0, 288)
    for l in range(2):
        prev = None
        for blk in range(18):
            W = B.wload([dr["w_mod"][l][:, blk * 512:(blk + 1) * 512]], (8, 512))
            for j in range(4):
                mc = blk * 4 + j
                for k in range(8):
                    B.mm(psv(l, mc * 2, 2), W.ap[:, k, j * 128:(j + 1) * 128], scv.ap[:, k, :], k == 0, k == 7,
                         reads=[W, scv], writes=[PS[l]])
        for ci in range(2):
            o = T("MT")[:, (l * 2 + ci) * 72:(l * 2 + ci + 1) * 72]
            i0 = PS[l].ap[:, 0:144].rearrange("p (m c) -> p m c", c=2)[:, :, ci]
            i1 = T("bmodT")[:, l * 72:(l + 1) * 72]
            B.dve(lambda e, o=o, i0=i0, i1=i1: e.tensor_tensor(o, i0, i1, ALU.add),
                  reads=[PS[l], cst["bmodT"]], writes=[MTv])
    Av = B.view("MA", 0, 96); Gv = B.view("MG", 0, 96)

    def mcol(l, ci, v, c):
        o = (l * 2 + ci) * 72 + v * 8 + c
        return T("MT")[:, o:o + 1]

    for l in range(2):
        for ci in range(2):
            for i in range(3):
                base = (l * 2 + ci) * 72
                o = ((l * 2 + ci) * 3 + i) * 8
                sc_in = T("MT")[:, base + (3 * i + 1) * 8: base + (3 * i + 2) * 8]
                g_in = T("MT")[:, base + (3 * i + 2) * 8: base + (3 * i + 3) * 8]
                B.dve(lambda e, o=o, a=sc_in: e.tensor_scalar(T("MA")[:, o:o + 8], a, 1.0, 1.0 / ALPHA, ALU.add, ALU.mult),
                      reads=[MTv], writes=[Av])
                gm = 1.0 if i == 1 else 0.5
                B.dve(lambda e, o=o, a=g_in, gm=gm: e.tensor_scalar(T("MG")[:, o:o + 8], a, gm, None, ALU.mult),
                      reads=[MTv], writes=[Gv])
    AGv = B.view("AG", 0, 48); ABv = B.view("AB", 0, 48)
    B.dve(lambda e: e.tensor_scalar(T("AG")[:, 0:48], T("lngT")[:, 0:48], ALPHA, None, ALU.mult), reads=[cst["lngT"]], writes=[AGv])
    B.dve(lambda e: e.tensor_scalar(T("AB")[:, 0:48], T("lnbT")[:, 0:48], ALPHA, None, ALU.mult), reads=[cst["lnbT"]], writes=[ABv])

    def A_(l, ci, i, c):
        o = ((l * 2 + ci) * 3 + i) * 8 + c
        return T("MA")[:, o:o + 1]

    def G_(l, ci, i, c):
        o = ((l * 2 + ci) * 3 + i) * 8 + c
        return T("MG")[:, o:o + 1]

    def SH_(l, ci, i, c):
        return mcol(l, ci, 3 * i, c)

    lamv = cst["lamv"]
    LMv = B.view("LM", 0, 16)
    for l in range(2):
        lam_init = 0.8 - 0.6 * math.exp(-0.3 * l)
        for j in range(2):
            a = T("LNS")[:, (l * 4 + 2 * j) * 64:(l * 4 + 2 * j + 1) * 64]
            b = T("LNS")[:, (l * 4 + 2 * j + 1) * 64:(l * 4 + 2 * j + 2) * 64]
            s = B.scratch(64)
            B.dve(lambda e, s=s, a=a, b=b: e.tensor_tensor(s.ap, a, b, ALU.mult), reads=[lamv], writes=[s])
            o = T("LM")[:, l * 8 + j: l * 8 + j + 1]
            B.dve(lambda e, s=s, o=o: e.tensor_reduce(o, s.ap, mybir.AxisListType.X, ALU.add), reads=[s], writes=[LMv])
            o2 = T("LM")[:, l * 8 + 2 + j: l * 8 + 3 + j]
            B.act(o2, o, AF.Exp, reads=[LMv], writes=[LMv])
        e1 = T("LM")[:, l * 8 + 2:l * 8 + 3]; e2 = T("LM")[:, l * 8 + 3:l * 8 + 4]; d_ = T("LM")[:, l * 8 + 4:l * 8 + 5]
        nl = T("LM")[:, l * 8 + 5:l * 8 + 6]
        B.dve(lambda e, e1=e1, e2=e2, d_=d_: e.tensor_tensor(d_, e2, e1, ALU.subtract), reads=[LMv], writes=[LMv])
        B.dve(lambda e, d_=d_, nl=nl, li=lam_init: e.tensor_scalar(nl, d_, -li, None, ALU.add), reads=[LMv], writes=[LMv])
        sg = T("sgT")[:, l:l + 1]
        B.dve(lambda e, sg=sg, li=lam_init: e.tensor_scalar(sg, sg, 1.0 - li, None, ALU.mult), reads=[lamv, cst["sgT"]], writes=[cst["sgT"]])

    def NEGLAM(l):
        return T("LM")[:, l * 8 + 5:l * 8 + 6]

    pname = ["-"]
    P.phase = "mod"

    toff = [0]

    def xa(c, tt, n=512):
        return B.view("XA", c * 2048 + (tt + toff[0]) * 512, n)

    def xa_half(hi):
        off = 1024 * hi
        full = B.view("XA", 0, 16384, dims=[8, 2048])
        keys = []
        for c_ in range(8):
            keys += B.view("XA", c_ * 2048 + off, 1024).keys
        return V(full.ap[:, :, off:off + 1024], keys)

    pend_ln = {}

    def ln_later(l, i, tt, final):
        pend_ln[tt] = (l, i, final)

    def alpha_scale(tt):
        for c in range(8):
            x = xa(c, tt)
            B.act(x.ap, x.ap, AF.Copy, reads=[x], writes=[x], scale=ALPHA)

    def ensure_ln(tt):
        if tt in pend_ln:
            ent = pend_ln.pop(tt)
            if ent[0] == "scale":
                alpha_scale(tt)
            else:
                l_, i_, f_ = ent
                ln_tile(l_, i_, tt, f_)

    def ln_tile(l, i, tt, final):
        old_phase = P.phase
        P.phase = P.phase.split("|")[0] + "|ln"
        S1, S2 = PS[5], PS[6]
        for c in range(8):
            x = xa(c, tt)
            sq = B.scratch()
            B.act(sq.ap, x.ap, AF.Square, reads=[x], writes=[sq])
            B.mm(psv(5), ones.ap, x.ap, c == 0, c == 7, reads=[ones, x], writes=[S1])
            B.mm(psv(6), ones.ap, sq.ap, c == 0, c == 7, reads=[ones, sq], writes=[S2])
        mu = B.view("LNS", 0, 512); rstd = B.view("LNS", 512, 512); mr = B.view("LNS", 1024, 512)
        B.dve(lambda e: e.tensor_copy(mu.ap, psv(5)), reads=[S1], writes=[mu])
        B.dve(lambda e: e.tensor_tensor(mr.ap, mu.ap, mu.ap, ALU.mult), reads=[mu], writes=[mr])
        B.dve(lambda e: e.tensor_tensor(rstd.ap, psv(6), mr.ap, ALU.subtract), reads=[S2, mr], writes=[rstd])
        B.act(rstd.ap, rstd.ap, AF.Sqrt, reads=[rstd, epsv], writes=[rstd], bias=T("EPS")[:, 0:1])
        B.dve(lambda e: e.reciprocal(rstd.ap, rstd.ap), reads=[rstd], writes=[rstd])
        B.dve(lambda e: e.tensor_tensor(mr.ap, mu.ap, rstd.ap, ALU.mult), reads=[mu, rstd], writes=[mr])
        t1s = {}

        def stage2(c):
            x = xa(c, tt)
            t1 = t1s.pop(c)
            B.dve(lambda e, t1=t1: e.tensor_tensor(t1.ap, t1.ap, mr.ap, ALU.subtract), reads=[t1, mr], writes=[t1])
            o = (l * 3 + i) * 8 + c
            if final:
                sc = T("lngT")[:, o:o + 1]; bi = T("lnbT")[:, o:o + 1]
                rd = [cst["lngT"], cst["lnbT"]]
            else:
                sc = T("AG")[:, o:o + 1]; bi = T("AB")[:, o:o + 1]
                rd = [AGv, ABv]
            B.act(x.ap, t1.ap, AF.Identity, reads=[t1] + rd, writes=[x], bias=bi, scale=sc)

        for c in range(8):
            x = xa(c, tt)
            t1 = B.scratch()
            t1s[c] = t1
            B.dve(lambda e, x=x, t1=t1: e.tensor_tensor(t1.ap, x.ap, rstd.ap, ALU.mult), reads=[x, rstd], writes=[t1])
            if c >= 1:
                stage2(c - 1)
        stage2(7)
        P.phase = old_phase

    def modx_into(dst, l, ci, i, c, tt):
        ensure_ln(tt)
        x = xa(c, tt)
        B.act(dst.ap, x.ap, AF.Identity, reads=[x, Av, MTv], writes=[dst], bias=SH_(l, ci, i, c), scale=A_(l, ci, i, c))

    def ffn(l, which, NT, ci, final, tgroups=None):
        i = 0 if which == 1 else 2
        if tgroups is None:
            tgroups = [[2 * hf, 2 * hf + 1] for hf in range(NT // 2)]
        w1 = dr["ffn%d_w1" % which][l]; w3 = dr["ffn%d_w3" % which][l]; w2 = dr["ffn%d_w2" % which][l]
        pair_i = 0
        bank_i = 0
        for tiles in tgroups:

            def mx(c, tl):
                return B.view("G", 11264 + c * 1024 + tl * 512, 512)

            def gg(fl, tl):
                return B.view("G", fl * 1024 + tl * 512, 512)

            for tl, tt in enumerate(tiles):
                for c in range(8):
                    modx_into(mx(c, tl), l, ci, i, c, tt)
            for ffh in range(2):
                P.phase = "%s.ffn%d|ph1" % (pname[0], which)
                blocks = [(0, 4), (4, 4), (8, 3)]
                loaded = {}

                def ld(bi):
                    f0, nf = blocks[bi]
                    cs = (ffh * 11 + f0) * 128
                    a = B.wload([w1[:, cs:cs + nf * 128]], (8, nf * 128))
                    b = B.wload([w3[:, cs:cs + nf * 128]], (8, nf * 128))
                    return a, b

                for bi in range(3):
                    for j in range(bi, bi + 1):
                        if j not in loaded:
                            loaded[j] = ld(j)
                    W1b, W3b = loaded[bi]
                    f0, nf = blocks[bi]
                    for tl, tt in enumerate(tiles):
                        for j in range(nf):
                            fl = f0 + j
                            b1 = (pair_i % 2) * 2; b3 = b1 + 1; pair_i += 1
                            for k in range(8):
                                m = mx(k, tl)
                                B.mm(psv(b1), W1b.ap[:, k, j * 128:(j + 1) * 128], m.ap, k == 0, k == 7, reads=[W1b, m], writes=[PS[b1]])
                            for k in range(8):
                                m = mx(k, tl)
                                B.mm(psv(b3), W3b.ap[:, k, j * 128:(j + 1) * 128], m.ap, k == 0, k == 7, reads=[W3b, m], writes=[PS[b3]])
                            s = B.scratch()
                            B.act(s.ap, psv(b1), AF.Silu, reads=[PS[b1]], writes=[s])
                            g = gg(fl, tl)
                            B.dve(lambda e, g=g, s=s, b3=b3: e.tensor_tensor(g.ap, s.ap, psv(b3), ALU.mult), reads=[s, PS[b3]], writes=[g])
                P.phase = "%s.ffn%d|ph2" % (pname[0], which)
                loaded = {}

                def ld2(dp):
                    r0 = ffh * 1408
                    return B.wload([w2[r0:r0 + 1408, dp * 256:(dp + 1) * 256]], (11, 256))

                for dp in range(4):
                    for j in range(dp, dp + 1):
                        if j not in loaded:
                            loaded[j] = ld2(j)
                    W2b = loaded[dp]
                    for dd in range(2):
                        d = dp * 2 + dd
                        for tl, tt in enumerate(tiles):
                            b = bank_i % 5; bank_i += 1
                            for fl in range(11):
                                g = gg(fl, tl)
                                B.mm(psv(b), W2b.ap[:, fl, dd * 128:(dd + 1) * 128], g.ap, fl == 0, fl == 10, reads=[W2b, g], writes=[PS[b]])
                            x = xa(d, tt)
                            B.dve(lambda e, x=x, b=b, d=d: e.scalar_tensor_tensor(x.ap, psv(b), G_(l, ci, i, d), x.ap, ALU.mult, ALU.add),
                                  reads=[PS[b], x, Gv], writes=[x])
            for tt in tiles:
                ln_later(l, i, tt, final)

    def mixer(l, NT, ci, sample, own=False):
        ntok = NT * 512
        nblk = NT * 4

        def yT(ch, tt):
            return B.view("G", ch * ntok + tt * 512, 512)

        def qT(pl=0, ph=128, c0=0, n=512):
            v = B.view("H", c0, n, pl=pl, ph=ph)
            return v

        def kT(pl=0, ph=128, c0=0, n=512):
            return B.view("H", 2048 + c0, n, pl=pl, ph=ph)

        def vaug(tb, c0=0, n=256):
            return B.view("H", 4608 + tb * 256 + c0, n)

        def aTM(tb):
            return B.view("H", 7424 + tb * 256, 256)

        ering = [0]

        def Ebuf():
            i_ = ering[0]; ering[0] = (i_ + 1) % 4
            return B.view("H", 11520 + i_ * 512, 512)

        if not sample:
            def mxv(c, tt):
                return B.view("G", 8 * ntok + c * ntok + tt * 512, 512)
            for tt in range(NT):
                for c in range(8):
                    modx_into(mxv(c, tt), l, ci, 1, c, tt)

            def get_modx(tt):
                return [mxv(c, tt) for c in range(8)]
        else:
            def get_modx(tt, buf=0):
                base = 16384 if buf == 0 else 4 * ntok
                vs = [B.view("G", base + c * 512, 512) for c in range(8)]
                for c in range(8):
                    modx_into(vs[c], l, ci, 1, c, tt)
                return vs

        win = dr["w_in"][l]
        bank = [0]
        wsel = cst["wsel"]

        def select_tiles(vf, ntl):
            v0 = vf(0)
            B.dve(lambda e, v0=v0: e.tensor_scalar(v0.ap, v0.ap, T("wsel")[:, 0:1], None, ALU.mult), reads=[v0, wsel], writes=[v0])
            for t_ in range(1, ntl):
                vt_ = vf(t_)
                B.dve(lambda e, v0=v0, vt_=vt_, t_=t_: e.scalar_tensor_tensor(v0.ap, vt_.ap, T("wsel")[:, t_:t_ + 1], v0.ap, ALU.mult, ALU.add),
                      reads=[v0, vt_, wsel], writes=[v0])

        def nb(n=5):
            b = bank[0] % n; bank[0] = (b + 1) % n
            return b

        P.phase = "%s.mix|pool" % pname[0]
        Wa = B.wload([win[:, 0:256]], (8, 256))
        for tt in range(NT):
            mxs = get_modx(tt)
            for tb4 in range(4):
                tb = tt * 4 + tb4
                b = nb()
                for k in range(8):
                    B.mm(psv(b, 0, 256), mxs[k].ap[:, tb4 * 128:(tb4 + 1) * 128], Wa.ap[:, k, :], k == 0, k == 7, reads=[mxs[k], Wa], writes=[PS[b]])
                a_ = aTM(tb)
                B.dve(lambda e, a_=a_, b=b: e.tensor_copy(a_.ap, psv(b, 0, 256)), reads=[PS[b]], writes=[a_])
        Ppv = cst["Pp"]; Psv = cst["Ps"]
        for tt in range(NT):
            for cch in range(2):
                b = nb()
                if sample:
                    for tb4 in range(4):
                        tb = tt * 4 + tb4
                        for g2 in range(2):
                            g = cch * 2 + g2
                            a_ = aTM(tb)
                            B.mm(psv(b, tb4 * 128, 128, g2 * 64, g2 * 64 + 64), a_.ap[:, g * 64:(g + 1) * 64],
                                 T("Ps")[:, g * 128:(g + 1) * 128], True, True, reads=[a_, Psv], writes=[PS[b]])
                else:
                    for sq in range(2):
                        seq = tt * 2 + sq
                        for g2 in range(2):
                            g = cch * 2 + g2
                            for sblk in range(2):
                                a_ = aTM(seq * 2 + sblk)
                                B.mm(psv(b, sq * 256, 256, g2 * 64, g2 * 64 + 64), a_.ap[:, g * 64:(g + 1) * 64],
                                     T("Pp")[:, (g * 2 + sblk) * 256:(g * 2 + sblk + 1) * 256], sblk == 0, sblk == 1,
                                     reads=[a_, Ppv], writes=[PS[b]])
                pT = B.bscratch()
                B.dve(lambda e, pT=pT, b=b: e.tensor_copy(pT.ap, psv(b)), reads=[PS[b]], writes=[pT])
                b2 = nb()
                B.mm(psv(b2), bdv.ap[:, l * 2 + cch, :], pT.ap, True, True, reads=[bdv, pT], writes=[PS[b2]])
                y = yT(6 + cch, tt)
                B.act(y.ap, psv(b2), AF.Identity, reads=[PS[b2], cst["poolsc"]], writes=[y], bias=0.0, scale=T("poolsc")[:, l * 2 + cch:l * 2 + cch + 1])

        class Step:
            __slots__ = ("pl", "ph", "kc0", "vt", "bias", "vc0", "oc0", "accO", "accS", "first", "last")

        def mkstep(**kw):
            st = Step()
            for k_, v_ in kw.items():
                setattr(st, k_, v_)
            return st

        def run_groups(groups, LA=2):
            flat = []
            for (nq, qc0, steps, post) in groups:
                for si, st in enumerate(steps):
                    flat.append((nq, qc0, st, post if si == len(steps) - 1 else None))
            pend = []
            issued_b = set()
            bcache = {}

            def do_score(nq, qc0, st):
                sb_ = nb(3)
                q_ = B.view("H", (0 if st.pl == 0 else 13568) + qc0, nq); k_ = kT(0, 128, st.kc0, 128)
                B.mm(psv(sb_, 0, nq), k_.ap, q_.ap, True, True, reads=[k_, q_], writes=[PS[sb_]])
                E = Ebuf()
                bias = bcache.pop(id(st), None)
                if bias is None:
                    B.act(E.ap[:, 0:nq], psv(sb_, 0, nq), AF.Exp, reads=[PS[sb_]], writes=[E], scale=0.125)
                else:
                    t = B.scratch()
                    B.dve(lambda e, t=t, sb_=sb_, bias=bias: e.scalar_tensor_tensor(t.ap, psv(sb_), 0.125, bias.ap, ALU.mult, ALU.add),
                          reads=[PS[sb_], bias], writes=[t])
                    B.act(E.ap, t.ap, AF.Exp, reads=[t], writes=[E])
                return E

            def do_pv(nq, st, E):
                vv = vaug(st.vt)
                B.mm(PS[st.accO].ap[:, 0:nq], vv.ap[:, st.vc0:st.vc0 + 128], E.ap[:, 0:nq], st.first, st.last,
                     reads=[E, vv, ("vcg",)], writes=[PS[st.accO]])
                B.mm(PS[st.accS].ap[:, 0:nq], T("onesb")[:, st.oc0:st.oc0 + 128], E.ap[:, 0:nq], st.first, st.last,
                     reads=[E, cst["onesb"]], writes=[PS[st.accS]])

            n = len(flat)
            for i in range(n + LA):
                for j_ in range(i, min(i + 3, n)):
                    stj = flat[j_][2]
                    if stj.bias is not None and j_ not in issued_b:
                        bcache[id(stj)] = stj.bias()
                        issued_b.add(j_)
                if i < n:
                    nq, qc0, st, _p = flat[i]
                    pend.append(do_score(nq, qc0, st))
                if i >= LA:
                    nq, qc0, st, post = flat[i - LA]
                    do_pv(nq, st, pend[i - LA])
                    if post is not None:
                        post()

        pending = [None]

        def flush_pending():
            if pending[0] is not None:
                pending[0]()
                pending[0] = None

        def make_finish(ytm, nq, ych, qc0):
            def fin():
                for qb in range(nq // 128):
                    B.P.add("pe", lambda e, qb=qb: e.transpose(PS7.ap[:, qb * 128:(qb + 1) * 128], ytm.ap[:, qb * 128:(qb + 1) * 128], ident.ap),
                            reads=[ytm, ident], writes=[PS7])
                yv = B.view("G", ych * ntok + qc0, nq)
                B.act(yv.ap, PS7.ap[:, 0:nq], AF.Copy, reads=[PS7], writes=[yv])
            return fin


        def qsplit(tt_):
            a_hi = B.view("H", tt_ * 512, 512, pl=64, ph=128)
            b_hi = B.view("H", 13568 + tt_ * 512, 512, pl=64, ph=128)
            B.act(b_hi.ap, a_hi.ap, AF.Copy, reads=[a_hi], writes=[b_hi])
            B.dve(lambda e, a_hi=a_hi: e.memset(a_hi.ap, 0.0), reads=[], writes=[a_hi])

        def qzero():
            b_lo = B.view("H", 13568, ntok, pl=0, ph=64)
            B.dve(lambda e, b_lo=b_lo: e.memset(b_lo.ap, 0.0), reads=[], writes=[b_lo])

        rope_tabs = {}
        mx_cache = {}
        if not sample:
            for tt in range(NT):
                mx_cache[tt] = get_modx(tt)

        nkt_lat = nblk
        for h in range(4):
            P.phase = "%s.mix|dproj" % pname[0]
            W1_ = B.wload([win[:, 256 + 128 * h:384 + 128 * h], win[:, 768 + 128 * h:896 + 128 * h],
                                        win[:, 1280 + 128 * h:1408 + 128 * h]], (8, 384))
            Wsw = None
            if sample:
                wsw = dr["w_in_sw"][l]
                Wsw = B.wload([wsw[:, 128 * h:128 * h + 128], wsw[:, 512 + 128 * h:640 + 128 * h]], (8, 256))
            qzero()
            for tt in range(NT):
                if sample:
                    if tt == 0:
                        mx_cache[0] = get_modx(0, 0)
                    if tt + 1 < NT:
                        mx_cache[tt + 1] = get_modx(tt + 1, (tt + 1) % 2)
                    Ct = B.view("FR", 0, 512); St = B.view("FR", 512, 512)
                    B.dma("sp", Ct.ap, dr["ropeC"][:, tt * 512:(tt + 1) * 512], ("fr", 0), reads=[], writes=[Ct])
                    B.dma("sp", St.ap, dr["ropeS"][:, tt * 512:(tt + 1) * 512], ("fr", 1), reads=[], writes=[St])
                    rope_tabs[tt] = (Ct, St)
                mxs = mx_cache[tt]
                for (wc0, dstf, rp, od) in (
                    (0, lambda t_: qT(0, 128, t_ * 512, 512), (Wsw, 0) if sample else None, None),
                    (128, lambda t_: kT(0, 128, t_ * 512, 512), (Wsw, 128) if sample else None,
                     (lambda t_: dr["ndkT"][l][h * 128:(h + 1) * 128, t_ * 512:(t_ + 1) * 512]) if not sample else None),
                ):
                    b = nb()
                    for k in range(8):
                        B.mm(psv(b), W1_.ap[:, k, wc0:wc0 + 128], mxs[k].ap, k == 0, k == 7, reads=[W1_, mxs[k]], writes=[PS[b]])
                    dst = dstf(tt)
                    if rp is None and od is not None:
                        s = B.scratch()
                        B.dve(lambda e, s=s, b=b: e.tensor_copy(s.ap, psv(b)), reads=[PS[b]], writes=[s])
                        B.act(dst.ap, s.ap, AF.Copy, reads=[s], writes=[dst])
                        B.dma("sp", od(tt), s.ap, ("scr", s.keys[0][1]), reads=[s], writes=[("out", "kv")])
                    elif rp is None:
                        B.act(dst.ap, psv(b), AF.Copy, reads=[PS[b]], writes=[dst])
                    else:
                        Wsw_, sc0 = rp
                        b2 = nb()
                        for k in range(8):
                            B.mm(psv(b2), Wsw_.ap[:, k, sc0:sc0 + 128], mxs[k].ap, k == 0, k == 7, reads=[Wsw_, mxs[k]], writes=[PS[b2]])
                        Ct, St = rope_tabs[tt]
                        t = B.scratch(); u = B.scratch()
                        B.dve(lambda e, t=t, b=b, Ct=Ct: e.tensor_tensor(t.ap, psv(b), Ct.ap, ALU.mult), reads=[PS[b], Ct], writes=[t])
                        B.dve(lambda e, u=u, b2=b2, St=St: e.tensor_tensor(u.ap, psv(b2), St.ap, ALU.mult), reads=[PS[b2], St], writes=[u])
                        B.dve(lambda e, t=t, u=u, dst=dst: e.tensor_tensor(dst.ap, t.ap, u.ap, ALU.add), reads=[t, u], writes=[dst])
                    if wc0 == 0:
                        qsplit(tt)
                for tb4 in range(4):
                    tb = tt * 4 + tb4
                    b = nb()
                    for k in range(8):
                        B.mm(psv(b, 0, 128), mxs[k].ap[:, tb4 * 128:(tb4 + 1) * 128], W1_.ap[:, k, 256:384], k == 0, k == 7,
                             reads=[mxs[k], W1_], writes=[PS[b]])
                    vv = vaug(tb)
                    if not sample:
                        s = B.scratch(128)
                        B.act(s.ap, psv(b, 0, 128), AF.Copy, reads=[PS[b]], writes=[s])
                        B.dve(lambda e, vv=vv, s=s: e.tensor_copy(vv.ap[:, 0:128], s.ap), reads=[s], writes=[vv])
                        B.dma("sp", dr["ndv"][l][tb * 128:(tb + 1) * 128, h * 128:(h + 1) * 128], s.ap, ("scr", s.keys[0][1]),
                              reads=[s], writes=[("out", "kv")])
                    else:
                        B.dve(lambda e, vv=vv, b=b: e.tensor_copy(vv.ap[:, 0:128], psv(b, 0, 128)), reads=[PS[b]], writes=[vv])
            if sample:
                kc = kT(0, 128, 2048, 512)
                s1 = B.scratch()
                B.dma("sp", s1.ap, dr["cdkT"][l][h], ("scr", s1.keys[0][1]), reads=[], writes=[s1])
                B.act(kc.ap, s1.ap, AF.Copy, reads=[s1], writes=[kc])
                s2 = B.scratch()
                B.dma("sp", s2.ap.rearrange("p (k d) -> p k d", k=4), dr["cdv"][l][:, h, :].rearrange("(k p) d -> p k d", p=128),
                      ("scr", s2.keys[0][1]), reads=[], writes=[s2])
                vall = B.view("H", 4608 + 16 * 256, 4 * 256, dims=[4, 256])
                B.dve(lambda e, vall=vall, s2=s2: e.tensor_copy(vall.ap[:, :, 0:128], s2.ap.rearrange("p (k d) -> p k d", k=4)),
                      reads=[s2], writes=[vall, ("vcg",)])
            P.phase = "%s.mix|dattn" % pname[0]
            if own:
                select_tiles(lambda t_: qT(0, 128, t_ * 512, 512), NT)
                select_tiles(lambda t_: B.view("H", 13568 + t_ * 512, 512), NT)
            if sample:
                groups = [(512, qt * 512, [(kt * 128, kt, None) for kt in range(20)]) for qt in range(1 if own else 4)]
            else:
                groups = [(256, sq * 256, [((2 * sq + j) * 128, 2 * sq + j, None) for j in range(2)]) for sq in range(4)]
            glist = []
            for (nq, qc0, ktl) in groups:
                steps = []
                for (pl, aO, aS) in ((0, 3, 4), (64, 5, 6)):
                    for ki, (kc0, vt, _b) in enumerate(ktl):
                        steps.append(mkstep(pl=pl, ph=pl + 64, kc0=kc0, vt=vt, bias=None, vc0=0, oc0=0, accO=aO, accS=aS,
                                            first=(ki == 0), last=(ki == len(ktl) - 1)))

                def post(nq=nq, qc0=qc0):
                    r0 = B.scratch(); r1 = B.scratch(); t = B.scratch(); o_ = B.scratch()
                    B.act(r0.ap[:, 0:nq], PS[4].ap[:, 0:nq], AF.Ln, reads=[PS[4]], writes=[r0])
                    B.act(r1.ap[:, 0:nq], PS[6].ap[:, 0:nq], AF.Ln, reads=[PS[6]], writes=[r1])
                    B.act(r0.ap[:, 0:nq], r0.ap[:, 0:nq], AF.Exp, reads=[r0], writes=[r0], scale=-1.0)
                    B.act(r1.ap[:, 0:nq], r1.ap[:, 0:nq], AF.Exp, reads=[r1], writes=[r1], scale=-1.0)
                    B.dve(lambda e, t=t, r1=r1: e.scalar_tensor_tensor(t.ap[:, 0:nq], PS[5].ap[:, 0:nq], NEGLAM(l), r1.ap[:, 0:nq], ALU.mult, ALU.mult),
                          reads=[PS[5], r1, LMv], writes=[t])
                    B.dve(lambda e, o_=o_, r0=r0: e.tensor_tensor(o_.ap[:, 0:nq], PS[3].ap[:, 0:nq], r0.ap[:, 0:nq], ALU.mult),
                          reads=[PS[3], r0], writes=[o_])
                    B.dve(lambda e, o_=o_, t=t: e.tensor_tensor(o_.ap[:, 0:nq], o_.ap[:, 0:nq], t.ap[:, 0:nq], ALU.add), reads=[o_, t], writes=[o_])
                    B.act(r1.ap[:, 0:nq], o_.ap[:, 0:nq], AF.Square, reads=[o_], writes=[r1])
                    mb = nb(3)
                    B.mm(psv(mb, 0, nq), ones.ap, r1.ap[:, 0:nq], True, True, reads=[ones, r1], writes=[PS[mb]])
                    B.act(r0.ap[:, 0:nq], psv(mb, 0, nq), AF.Ln, reads=[PS[mb], epsv], writes=[r0], bias=T("EPS")[:, 1:2], scale=8.0)
                    B.act(r0.ap[:, 0:nq], r0.ap[:, 0:nq], AF.Exp, reads=[r0], writes=[r0], scale=-0.5)
                    yv = B.view("G", h * ntok + qc0, nq)
                    B.dve(lambda e, yv=yv, o_=o_, r0=r0: e.scalar_tensor_tensor(yv.ap, o_.ap[:, 0:nq], T("sgT")[:, l:l + 1], r0.ap[:, 0:nq], ALU.mult, ALU.mult),
                          reads=[o_, r0, cst["sgT"]], writes=[yv])
                glist.append((nq, qc0, steps, post))
            run_groups(glist)

        for cch in range(2):
            P.phase = "%s.mix|nproj" % pname[0]
            W1_ = B.wload([win[:, 1792 + 128 * cch:1920 + 128 * cch], win[:, 2048 + 128 * cch:2176 + 128 * cch],
                                          win[:, 2304 + 128 * cch:2432 + 128 * cch]], (8, 384))
            qzero()
            for tb in range(20 if sample else nblk):
                vv = vaug(tb)
                B.dve(lambda e, vv=vv: e.memset(vv.ap[:, 64:192], 0.0), reads=[], writes=[vv])
            for tt in range(NT):
                if sample:
                    mx_cache[tt] = get_modx(tt)
                mxs = mx_cache[tt]
                for (wc0, dstf, od) in (
                    (0, lambda t_: qT(0, 128, t_ * 512, 512), None),
                    (128, lambda t_: kT(0, 128, t_ * 512, 512),
                     (lambda t_: dr["nnkT"][l][cch * 128:(cch + 1) * 128, t_ * 512:(t_ + 1) * 512]) if not sample else None),
                ):
                    b = nb()
                    for k in range(8):
                        B.mm(psv(b), W1_.ap[:, k, wc0:wc0 + 128], mxs[k].ap, k == 0, k == 7, reads=[W1_, mxs[k]], writes=[PS[b]])
                    dst = dstf(tt)
                    if od is not None:
                        s = B.scratch()
                        B.dve(lambda e, s=s, b=b: e.tensor_copy(s.ap, psv(b)), reads=[PS[b]], writes=[s])
                        B.act(dst.ap, s.ap, AF.Copy, reads=[s], writes=[dst])
                        B.dma("sp", od(tt), s.ap, ("scr", s.keys[0][1]), reads=[s], writes=[("out", "kv")])
                    else:
                        B.act(dst.ap, psv(b), AF.Copy, reads=[PS[b]], writes=[dst])
                    if wc0 == 0:
                        qsplit(tt)
                for tb4 in range(4):
                    tb = tt * 4 + tb4
                    b = nb()
                    for k in range(8):
                        B.mm(psv(b, 0, 128), mxs[k].ap[:, tb4 * 128:(tb4 + 1) * 128], W1_.ap[:, k, 256:384], k == 0, k == 7,
                             reads=[mxs[k], W1_], writes=[PS[b]])
                    vv = vaug(tb)
                    if not sample:
                        s = B.scratch(128)
                        B.act(s.ap, psv(b, 0, 128), AF.Copy, reads=[PS[b]], writes=[s])
                        B.dve(lambda e, vv=vv, s=s: e.tensor_copy(vv.ap[:, 0:64], s.ap[:, 0:64]), reads=[s], writes=[vv])
                        B.dve(lambda e, vv=vv, s=s: e.tensor_copy(vv.ap[:, 192:256], s.ap[:, 64:128]), reads=[s], writes=[vv])
                        B.dma("sp", dr["nnv"][l][tb * 128:(tb + 1) * 128, cch * 128:(cch + 1) * 128], s.ap, ("scr", s.keys[0][1]),
                              reads=[s], writes=[("out", "kv")])
                    else:
                        B.dve(lambda e, vv=vv, b=b: e.tensor_copy(vv.ap[:, 0:64], psv(b, 0, 64)), reads=[PS[b]], writes=[vv])
                        B.dve(lambda e, vv=vv, b=b: e.tensor_copy(vv.ap[:, 192:256], psv(b, 64, 64)), reads=[PS[b]], writes=[vv])
            if sample:
                kc = kT(0, 128, 2048, 512)
                s1 = B.scratch()
                B.dma("sp", s1.ap, dr["cnkT"][l][cch], ("scr", s1.keys[0][1]), reads=[], writes=[s1])
                B.act(kc.ap, s1.ap, AF.Copy, reads=[s1], writes=[kc])
                s2 = B.scratch()
                B.dma("sp", s2.ap.rearrange("p (k h d) -> p k h d", k=4, h=2),
                      dr["cnv"][l][:, 2 * cch:2 * cch + 2, :].rearrange("(k p) h d -> p k h d", p=128),
                      ("scr", s2.keys[0][1]), reads=[], writes=[s2])
                vall = B.view("H", 4608 + 16 * 256, 4 * 256, dims=[4, 256])
                s2v = s2.ap.rearrange("p (k h d) -> p k h d", k=4, h=2)
                B.dve(lambda e, vall=vall, s2v=s2v: e.tensor_copy(vall.ap[:, :, 0:64], s2v[:, :, 0, :]), reads=[s2], writes=[vall, ("vcg",)])
                B.dve(lambda e, vall=vall, s2v=s2v: e.tensor_copy(vall.ap[:, :, 192:256], s2v[:, :, 1, :]), reads=[s2], writes=[vall, ("vcg",)])
            biasctr = [0]
            P.phase = "%s.mix|nattn" % pname[0]
            glist = []
            for gi in range(4):
                if sample:
                    nq = 512; qc0 = gi * 512
                    lo = max(8 * gi - 4, 0); hi = min(8 * gi + 12, 32)
                else:
                    nq = 256; qc0 = gi * 256
                steps = []
                for hh in range(2):
                    head = 2 * cch + hh
                    if sample:
                        ktl = [((kr // 2) * 128, kr // 2, j) for j, kr in enumerate(range(lo, hi, 2))]
                        ktl += [(2048 + kt * 128, 16 + kt, None) for kt in range(4)]
                    else:
                        ktl = [((2 * gi + j) * 128, 2 * gi + j, None) for j in range(2)]
                    for ki, (kc0, vt, j) in enumerate(ktl):
                        bias = None
                        if j is not None:
                            def bias(head=head, gi=gi, j=j):
                                slot = biasctr[0] % 4; biasctr[0] += 1
                                bt = B.view("FR", slot * 512, 512)
                                B.dma("sp", bt.ap, dr["nabias"][l][head][gi][j], ("fr", slot), reads=[], writes=[bt])
                                return bt
                        steps.append(mkstep(pl=hh * 64, ph=hh * 64 + 64, kc0=kc0, vt=vt, bias=bias, vc0=hh * 128, oc0=128 + hh * 128,
                                            accO=3, accS=4, first=(hh == 0 and ki == 0), last=(hh == 1 and ki == len(ktl) - 1)))

                def post(nq=nq, qc0=qc0):
                    r = B.scratch()
                    B.act(r.ap[:, 0:nq], PS[4].ap[:, 0:nq], AF.Ln, reads=[PS[4]], writes=[r])
                    B.act(r.ap[:, 0:nq], r.ap[:, 0:nq], AF.Exp, reads=[r], writes=[r], scale=-1.0)
                    yv = B.view("G", (4 + cch) * ntok + qc0, nq)
                    B.dve(lambda e, yv=yv, r=r: e.tensor_tensor(yv.ap, PS[3].ap[:, 0:nq], r.ap[:, 0:nq], ALU.mult), reads=[PS[3], r], writes=[yv])
                glist.append((nq, qc0, steps, post))
            run_groups(glist)

        if own:
            P.phase = "%s.mix|select" % pname[0]
            for ch in range(4, 8):
                select_tiles(lambda t_, ch=ch: yT(ch, t_), NT)
            for t_ in range(NT):
                ensure_ln(t_)
            for c in range(8):
                select_tiles(lambda t_, c=c: xa(c, t_), NT)
        for tiles in ([[0]] if own else [[2 * hf, 2 * hf + 1] for hf in range(NT // 2)]):
            P.phase = "%s.mix|merge" % pname[0]

            def mxh(c, tl):
                return B.view("H", 8192 + c * 1024 + tl * 512, 512)

            def mixed(c, tl):
                return B.view("H", c * 1024 + tl * 512, 512)

            for tl, tt in enumerate(tiles):
                for c in range(8):
                    modx_into(mxh(c, tl), l, ci, 1, c, tt)
            Wpac = B.wload([(dr["w_pa"][l], 0, 0), (dr["w_pc"][l], 2, 0)], (4, 1024))
            Wpb = B.wload([dr["w_pb"][l]], (4, 1024))
            loaded = {}

            def ldg(d):
                return B.wload([win[:, 2560 + br * 1024 + d * 128:2560 + br * 1024 + (d + 1) * 128] for br in range(3)], (8, 384))

            pair = [0]
            for d in range(8):
                for j in range(d, d + 1):
                    if j not in loaded:
                        loaded[j] = ldg(j)
                Wg = loaded[d]
                for tl, tt in enumerate(tiles):
                    acc = B.scratch()
                    for br in range(3):
                        bg = (pair[0] % 3) * 2; bp = bg + 1; pair[0] += 1
                        for k in range(8):
                            m = mxh(k, tl)
                            B.mm(psv(bg), Wg.ap[:, k, br * 128:(br + 1) * 128], m.ap, k == 0, k == 7, reads=[Wg, m], writes=[PS[bg]])
                        if br == 0:
                            chs = [(Wpac.ap[:, kc, d * 128:(d + 1) * 128], yT(6 + kc, tt), Wpac) for kc in range(2)]
                        elif br == 1:
                            chs = [(Wpb.ap[:, kc, d * 128:(d + 1) * 128], yT(kc, tt), Wpb) for kc in range(4)]
                        else:
                            chs = [(Wpac.ap[:, 2 + kc, d * 128:(d + 1) * 128], yT(4 + kc, tt), Wpac) for kc in range(2)]
                        for ii, (wap, yv, wv) in enumerate(chs):
                            B.mm(psv(bp), wap, yv.ap, ii == 0, ii == len(chs) - 1, reads=[wv, yv], writes=[PS[bp]])
                        sg = B.scratch()
                        B.act(sg.ap, psv(bg), AF.Sigmoid, reads=[PS[bg]], writes=[sg])
                        if br == 0:
                            B.dve(lambda e, acc=acc, sg=sg, bp=bp: e.tensor_tensor(acc.ap, sg.ap, psv(bp), ALU.mult), reads=[sg, PS[bp]], writes=[acc])
                        else:
                            B.dve(lambda e, sg=sg, bp=bp: e.tensor_tensor(sg.ap, sg.ap, psv(bp), ALU.mult), reads=[sg, PS[bp]], writes=[sg])
                            dstv = mixed(d, tl) if br == 2 else acc
                            B.dve(lambda e, acc=acc, sg=sg, dstv=dstv: e.tensor_tensor(dstv.ap, acc.ap, sg.ap, ALU.add), reads=[acc, sg], writes=[dstv])
            P.phase = "%s.mix|wout" % pname[0]
            wo = dr["w_out"][l]
            loaded = {}
            for dp in range(4):
                for j in range(dp, dp + 1):
                    if j not in loaded:
                        loaded[j] = B.wload([wo[:, j * 256:(j + 1) * 256]], (8, 256))
                Wo = loaded[dp]
                for dd in range(2):
                    d = dp * 2 + dd
                    for tl, tt in enumerate(tiles):
                        b = nb()
                        for k in range(8):
                            m = mixed(k, tl)
                            B.mm(psv(b), Wo.ap[:, k, dd * 128:(dd + 1) * 128], m.ap, k == 0, k == 7, reads=[Wo, m], writes=[PS[b]])
                        x = xa(d, tt)
                        B.dve(lambda e, x=x, b=b, d=d: e.scalar_tensor_tensor(x.ap, psv(b), G_(l, ci, 1, d), x.ap, ALU.mult, ALU.add),
                              reads=[PS[b], x, Gv], writes=[x])
            for tt in tiles:
                ln_later(l, 1, tt, False)

    import os
    kpass = os.environ.get("KPASS", "ps")
    klayers = int(os.environ.get("KLAYERS", "2"))
    for pas, (xname, yname, NT, ci) in enumerate((("xpT", "ypT", 2, 0), ("xsT", "ysT", 4, 1))):
        if "ps"[pas] not in kpass:
            continue
        pname[0] = "PS"[pas]
        ntok = NT * 512
        xin = dr[xname].rearrange("(c p) t -> p c t", p=128)
        if pas == 0:
            toff[0] = 2
            up = xa_half(1)
            B.dma("sp", up.ap, xin, "xa_in", reads=[], writes=[up])
            if "s" in kpass:
                lo_ = xa_half(0)
                xs_in = dr["xsT"].rearrange("(c p) t -> p c t", p=128)
                B.dma("sp", lo_.ap, xs_in[:, :, 0:1024], "xa_in2", reads=[], writes=[lo_])
        else:
            toff[0] = 0
            if "p" not in kpass:
                lo_ = xa_half(0)
                B.dma("sp", lo_.ap, xin[:, :, 0:1024], "xa_in2", reads=[], writes=[lo_])
            up = xa_half(1)
            B.dma("sp", up.ap, xin[:, :, 1024:2048], "xa_in", reads=[], writes=[up])
        for tt in range(NT):
            if pas == 1 and tt >= 2:
                pend_ln[tt] = ("scale",)
            else:
                alpha_scale(tt)
        import os
        stage = int(os.environ.get("KSTAGE", "9"))
        for l in range(klayers):
            if stage >= 2:
                ffn(l, 1, NT, ci, False)
            own = (pas == 1 and l == klayers - 1 and os.environ.get("KOWN", "1") == "1")
            if stage >= 3:
                mixer(l, NT, ci, pas == 1, own)
            if stage >= 2:
                ffn(l, 2, NT, ci, l == 1, [[0]] if own else None)
        for t_ in range(4):
            ensure_ln(t_)
        yout = dr[yname].rearrange("(c p) t -> p c t", p=128)
        if pas == 0:
            up = xa_half(1)
            B.dma("sp", yout, up.ap, "xa_out", reads=[up], writes=[("out", "y")])
        else:
            lo_ = xa_half(0)
            B.dma("sp", yout, lo_.ap[:, :, 0:512], "xa_out", reads=[lo_], writes=[("out", "y")])
    fin = P.add("sp", None, reads=[("out", "y")], writes=[])
    seen = {}
    for op in P.ops["sp"]:
        if op.dma is not None and op.dma[0][0] in ("scr",) or (op.dma is not None and op.dma[0] == "xa_out"):
            seen[op.dma[0]] = op
    fin.deps = list(seen.values())


SBUF_SPECS = [
    ("XA", 16384, F32), ("G", 20480, BF16), ("H", 16384, BF16), ("WR", 16384, BF16),
    ("SCR", 5 * 512, F32), ("BS", 3 * 512, BF16), ("FR", 2048, F32), ("LNS", 1536, F32),
    ("condT", 16, F32), ("bmodT", 144, F32), ("lngT", 48, F32), ("lnbT", 48, F32), ("poolsc", 4, F32),
    ("sgT", 2, F32), ("onesf", 128, F32), ("onesb", 384, BF16),
    ("Pp", 2048, BF16), ("Ps", 512, BF16), ("poolbd", 512, BF16), ("scb", 16, BF16),
    ("MT", 288, F32), ("MA", 96, F32), ("MG", 96, F32), ("AG", 48, F32), ("AB", 48, F32), ("LM", 16, F32), ("SM", 8, F32), ("EPS", 2, F32), ("wsel", 4, F32),
]


def build_nc():
    from contextlib import ExitStack
    nc = bass.Bass("TRN2", target_bir_lowering=False)
    B = build(nc)
    with ExitStack() as es:
        sb = {}
        for name, n, dt in SBUF_SPECS:
            t = es.enter_context(nc.sbuf_tensor("sb_" + name, [128, n], dt))
            sb[name] = (t, 4 if dt == F32 else 2)
        ps = []
        for i in range(7):
            ps.append(es.enter_context(nc.psum_tensor("ps%d" % i, [128, 512], F32)))
        ps.append(es.enter_context(nc.psum_tensor("ps7", [128, 1024], BF16)))
        B.sb = sb
        B.wq_init(None)
        emit_all(B, nc, sb, ps)
        rec = (B.wq_rec, B.wq_last)
        B.P = Prog()
        B.scr_i = 0
        B.bs_i = 0
        B.wq_init(rec)
        emit_all(B, nc, sb, ps)
        P = B.P
        import os, json
        if os.environ.get("KTAGS"):
            json.dump(P.tags, open(os.environ["KTAGS"], "w"))
        esem = {e: es.enter_context(nc.semaphore("se_" + e)) for e in Prog.ENGS}
        dsem = {}
        for k in P.dcount:
            dsem[k] = es.enter_context(nc.semaphore("sd_%d" % len(dsem)))
        block = es.enter_context(nc.Block())
        engs = {}

        def mk(ename):
            def f(eng):
                engs[ename] = eng
                P_emit_one(P, ename, eng, esem, dsem)
            return f
        block.tensor(mk("pe"))
        block.scalar(mk("act"))
        block.vector(mk("dve"))
        block.gpsimd(mk("pool"))
        block.sync(mk("sp"))
    return nc


def P_emit_one(P, e, eng, esem, dsem):
    if not getattr(P, "_sig_done", False):
        for e2 in P.ENGS:
            c = 0
            for op in P.ops[e2]:
                if op.dma is None and op.signal:
                    c += 1
                    op.sigval = c
        P._sig_done = True
    waited = {}
    for op in P.ops[e]:
        need = {}
        for d in op.deps:
            if d.dma is not None:
                key = ("d", d.dma[0]); val = d.dma[1]; sem = dsem[d.dma[0]]
            else:
                key = ("e", d.eng); val = d.sigval; sem = esem[d.eng]
            if waited.get(key, 0) >= val:
                continue
            if key not in need or need[key][1] < val:
                need[key] = (sem, val)
        for key, (sem, val) in need.items():
            eng.wait_ge(sem, val)
            waited[key] = val
        if op.fn is None:
            continue
        ins = op.fn(eng)
        if op.dma is not None:
            ins.then_inc(dsem[op.dma[0]], 16)
        elif op.signal:
            ins.then_inc(esem[e], 1)


def _consts():
    def pmat(n, w):
        t = np.arange(n)
        lo = np.clip(t - w // 2, 0, n); hi = np.clip(t + w // 2, 0, n)
        s = np.arange(n)[:, None]
        M = ((s >= lo[None, :]) & (s < hi[None, :])).astype(np.float64) / (hi - lo)[None, :]
        M -= np.eye(n)
        return M.astype(np.float32)
    wins = (2, 4, 8, 16)
    Pp = np.zeros((128, 4, 2, 256), np.float32)
    Ps = np.zeros((128, 4, 128), np.float32)
    for g, w in enumerate(wins):
        M = pmat(256, w)
        for sblk in range(2):
            Pp[:, g, sblk, :] = M[sblk * 128:(sblk + 1) * 128, :]
        M64 = pmat(64, w)
        Ps[0:64, g, 0:64] = M64
        Ps[64:128, g, 64:128] = M64
    t = np.arange(2048)
    row = (t // 64).astype(np.float32); col = (t % 64).astype(np.float32)
    inv = (10000.0 ** (-np.arange(16, dtype=np.float32) / 16)).astype(np.float32)
    C = np.zeros((128, 2048), np.float32); S = np.zeros((128, 2048), np.float32)
    for p in range(128):
        dd = p % 64
        pos = row if dd < 32 else col
        j = dd % 32
        ang = (pos * inv[j % 16]).astype(np.float32)
        C[p] = np.cos(ang)
        S[p] = -np.sin(ang) if j < 16 else np.sin(ang)
    swap = np.array([(d + 16) if (d % 32) < 16 else (d - 16) for d in range(64)])
    return Pp.reshape(128, -1), Ps.reshape(128, -1), C, S, swap


def _na_bias(na_rpb):
    col = np.arange(64)
    col_start = np.clip(col - 8, 0, 48)
    ok = (col[None, :] >= col_start[:, None]) & (col[None, :] < col_start[:, None] + 16)
    dc = np.clip(col[None, :] - col[:, None], -15, 15) + 15
    out = np.full((2, 4, 4, 8, 128, 512), NEG, np.float32)
    for qt in range(4):
        lo = max(8 * qt - 4, 0); hi = min(8 * qt + 12, 32)
        for j, kr0 in enumerate(range(lo, hi, 2)):
            for a in range(2):
                kr = kr0 + a
                for i in range(8):
                    qr = 8 * qt + i
                    rs = min(max(qr - 4, 0), 24)
                    if not (rs <= kr < rs + 8):
                        continue
                    drr = kr - qr + 7
                    blk = na_rpb[:, :, drr, :][:, :, dc]
                    blk = np.where(ok[None, None], blk, np.float32(NEG))
                    out[:, :, qt, j, a * 64:(a + 1) * 64, i * 64:(i + 1) * 64] = blk.transpose(0, 1, 3, 2)
    return out


def _onesb():
    o = np.zeros((128, 384), np.float32)
    o[:, 0:128] = 1.0
    o[:, 128:192] = 1.0
    o[:, 320:384] = 1.0
    return o


_NC_CACHE = {}
_PREP_ONLY = False


def kernel(x_prompt, x_sample, cache_diff_k, cache_diff_v, cache_na_k, cache_na_v, c, c_ctx,
           w_mod, b_mod, ln_g, ln_b, ffn1_w1, ffn1_w3, ffn1_w2, ffn2_w1, ffn2_w3, ffn2_w2,
           w_in, pool_w, pool_scale, w_pa, w_pb, w_pc, lam_q1, lam_k1, lam_q2, lam_k2,
           subln_g, na_rpb, w_out):
    f = lambda a: np.ascontiguousarray(np.asarray(a, dtype=np.float32))
    x_prompt, x_sample = f(x_prompt), f(x_sample)
    Pp, Ps, C, S, swap = _consts()
    w_in = f(w_in)
    qcols = np.concatenate([256 + 64 * blk + swap for blk in range(8)])
    kcols = qcols + 512
    w_in_sw = f(w_in[:, :, np.concatenate([qcols, kcols])])
    poolbd = np.zeros((2, 2, 128, 128), np.float32)
    pw = f(pool_w)
    for l in range(2):
        for cc in range(2):
            poolbd[l, cc, 0:64, 0:64] = pw[l, 2 * cc]
            poolbd[l, cc, 64:128, 64:128] = pw[l, 2 * cc + 1]
    common = {
        "bmodT": f(f(b_mod).reshape(2, 72, 128).transpose(2, 0, 1).reshape(128, 144)),
        "lngT": f(f(ln_g).reshape(2, 3, 8, 128).transpose(3, 0, 1, 2).reshape(128, 48)),
        "lnbT": f(f(ln_b).reshape(2, 3, 8, 128).transpose(3, 0, 1, 2).reshape(128, 48)),
        "w_mod": f(w_mod), "ffn1_w1": f(ffn1_w1), "ffn1_w3": f(ffn1_w3), "ffn1_w2": f(ffn1_w2),
        "ffn2_w1": f(ffn2_w1), "ffn2_w3": f(ffn2_w3), "ffn2_w2": f(ffn2_w2),
        "w_in": w_in, "w_in_sw": w_in_sw, "w_pa": f(w_pa), "w_pb": f(w_pb), "w_pc": f(w_pc), "w_out": f(w_out),
        "poolbd": poolbd,
        "poolsc": f(f(pool_scale).reshape(2, 2, 128).transpose(2, 0, 1).reshape(128, 4)),
        "Pp": Pp, "Ps": Ps,
        "lamv": f(np.broadcast_to(np.stack([f(lam_q1), f(lam_k1), f(lam_q2), f(lam_k2)], axis=1).reshape(1, 512), (128, 512))),
        "sgT": f(f(subln_g).T),
        "ropeC": C, "ropeS": S, "nabias": _na_bias(f(na_rpb)),
        "onesb": _onesb(), "onesf": np.full((128, 128), 1.0 / 1024.0, np.float32),
    }
    cdk, cdv, cnk, cnv = f(cache_diff_k), f(cache_diff_v), f(cache_na_k), f(cache_na_v)
    cf, cctx = f(c), f(c_ctx)
    in_maps = []
    for core in range(8):
        b = core // 4
        m = dict(common)
        m["xpT"] = f(x_prompt[4 * core:4 * core + 4].reshape(1024, 1024).T)
        m["xsT"] = f(x_sample[b].T)
        cond = np.stack([cctx, cf[b]], axis=0)
        m["condT"] = f(cond.reshape(2, 8, 128).transpose(2, 1, 0).reshape(128, 16))
        m["cdkT"] = f(cdk[b].reshape(2, 512, 4, 128).transpose(0, 2, 3, 1))
        m["cdv"] = f(cdv[b])
        m["cnkT"] = f(cnk[b].reshape(2, 512, 2, 128).transpose(0, 2, 3, 1))
        m["cnv"] = f(cnv[b])
        ws = np.zeros((128, 4), np.float32); ws[:, core % 4] = 1.0
        m["wsel"] = ws
        in_maps.append(m)
    if _PREP_ONLY:
        return in_maps
    if "nc" not in _NC_CACHE:
        _NC_CACHE["nc"] = build_nc()
    nc = _NC_CACHE["nc"]
    res = run_bass_kernel_spmd(nc, in_maps, core_ids=list(range(8)))
    R = res.results
    return _assemble(R)


def _assemble(R):
    y_prompt = np.concatenate([np.asarray(R[i]["ypT"]).T.reshape(4, 256, 1024) for i in range(8)], axis=0)
    y_sample = np.stack([np.concatenate([np.asarray(R[4 * b_ + j]["ysT"]).T for j in range(4)], axis=0) for b_ in range(2)], axis=0)
    ndk = np.concatenate([np.asarray(R[i]["ndkT"]).transpose(2, 0, 1).reshape(4, 256, 2, 4, 2, 64).transpose(0, 2, 1, 3, 4, 5)
                          for i in range(8)], axis=0)
    ndv = np.concatenate([np.asarray(R[i]["ndv"]).reshape(2, 4, 256, 4, 128).transpose(1, 0, 2, 3, 4) for i in range(8)], axis=0)
    nnk = np.concatenate([np.asarray(R[i]["nnkT"]).transpose(2, 0, 1).reshape(4, 256, 2, 4, 64).transpose(0, 2, 1, 3, 4)
                          for i in range(8)], axis=0)
    nnv = np.concatenate([np.asarray(R[i]["nnv"]).reshape(2, 4, 256, 4, 64).transpose(1, 0, 2, 3, 4) for i in range(8)], axis=0)
    o = lambda a: np.ascontiguousarray(a, dtype=np.float32)
    return (o(y_prompt), o(y_sample), o(ndk), o(ndv), o(nnk), o(nnv))
```

```python
import math
import numpy as np
import concourse.bass as bass
import concourse.mybir as mybir
from concourse.bass_utils import run_bass_kernel_spmd

F32 = mybir.dt.float32
BF16 = mybir.dt.bfloat16
AF = mybir.ActivationFunctionType
ALU = mybir.AluOpType

D = 1024
DFF = 2816
DIN = 5632
ALPHA = 4.0 ** 0.25
LN_EPS = 1e-5
RMS_EPS = 1e-5
NEG = -1e30
CELL = 1024


class Op:
    __slots__ = ("eng", "fn", "deps", "signal", "dma", "idx", "sigval")

    def __init__(self, eng, fn, dma):
        self.eng = eng
        self.fn = fn
        self.deps = []
        self.signal = False
        self.dma = dma
        self.idx = -1
        self.sigval = 0


class V:
    __slots__ = ("ap", "keys", "blk")

    def __init__(self, ap, keys):
        self.ap = ap
        self.keys = keys
        self.blk = None


class Prog:
    ENGS = ("pe", "act", "dve", "pool", "sp")

    def __init__(self):
        self.ops = {e: [] for e in self.ENGS}
        self.lastw = {}
        self.readers = {}
        self.dcount = {}
        self.tags = []
        self.phase = "init"

    def add(self, eng, fn, reads=(), writes=(), dma=None):
        if dma is not None:
            c = self.dcount.get(dma, 0) + 16
            self.dcount[dma] = c
            dma = (dma, c)
        op = Op(eng, fn, dma)
        deps = {}
        rk = []
        for r in reads:
            rk.extend(r.keys if isinstance(r, V) else [r])
        wk = []
        for w in writes:
            wk.extend(w.keys if isinstance(w, V) else [w])
        for k in rk:
            if isinstance(k, tuple) and k[0] == "ps" and k not in wk:
                wk.append(k)
        for k in rk:
            w = self.lastw.get(k)
            if w is not None:
                deps[id(w)] = (w, True)
        for k in wk:
            for rd in self.readers.get(k, ()):
                if id(rd) not in deps:
                    deps[id(rd)] = (rd, False)
            w = self.lastw.get(k)
            if w is not None and id(w) not in deps:
                deps[id(w)] = (w, False)
        for d, raw in deps.values():
            if d.dma is None and d.eng == eng:
                if eng == "pe":
                    continue
            if d.dma is None:
                d.signal = True
            op.deps.append(d)
        op.idx = len(self.ops[eng])
        self.ops[eng].append(op)
        if eng == "pe":
            self.tags.append(self.phase)
        for k in rk:
            self.readers.setdefault(k, []).append(op)
        for k in wk:
            self.lastw[k] = op
            self.readers[k] = []
        return op

    def emit(self, nc, block_engines, esem, dsem):
        for e in self.ENGS:
            c = 0
            for op in self.ops[e]:
                if op.dma is None and op.signal:
                    c += 1
                    op.sigval = c
        for e in self.ENGS:
            eng = block_engines[e]
            waited = {}
            for op in self.ops[e]:
                need = {}
                for d in op.deps:
                    if d.dma is not None:
                        key = ("d", d.dma[0])
                        val = d.dma[1]
                        sem = dsem[d.dma[0]]
                    else:
                        key = ("e", d.eng)
                        val = d.sigval
                        sem = esem[d.eng]
                    if waited.get(key, 0) >= val:
                        continue
                    if key not in need or need[key][1] < val:
                        need[key] = (sem, val)
                for key, (sem, val) in need.items():
                    eng.wait_ge(sem, val)
                    waited[key] = val
                if op.fn is None:
                    continue
                ins = op.fn(eng)
                if op.dma is not None:
                    ins.then_inc(dsem[op.dma[0]], 16)
                elif op.signal:
                    ins.then_inc(esem[e], 1)


class Builder:
    def __init__(self, nc):
        self.nc = nc
        self.P = Prog()
        self.dr = {}
        self.scr_i = 0
        self.bs_i = 0

    def din(self, name, shape):
        self.dr[name] = self.nc.dram_tensor(name, list(shape), F32, kind="ExternalInput").ap()
        return self.dr[name]

    def dout(self, name, shape):
        self.dr[name] = self.nc.dram_tensor(name, list(shape), F32, kind="ExternalOutput").ap()
        return self.dr[name]

    def view(self, tname, off, n, dims=None, pl=0, ph=128):
        t, esz = self.sb[tname]
        ap = t[pl:ph, off:off + n]
        if dims is not None:
            names = " ".join("d%d" % i for i in range(len(dims)))
            kw = {"d%d" % i: dims[i] for i in range(len(dims))}
            ap = ap.rearrange("p (%s) -> p %s" % (names, names), **kw)
        b0 = off * esz
        b1 = (off + n) * esz
        keys = [(tname, c) for c in range(b0 // CELL, (b1 - 1) // CELL + 1)]
        return V(ap, keys)

    def scratch(self, n=512):
        i = self.scr_i
        self.scr_i = (i + 1) % self.NSCR
        return self.view("SCR", i * 512, n)

    def bscratch(self, n=512):
        i = self.bs_i
        self.bs_i = (i + 1) % self.NBS
        return self.view("BS", i * 512, n)

    def mm(self, out, lhsT, rhs, start, stop, reads, writes):
        self.P.add("pe", lambda e: e.matmul(out, lhsT, rhs, start=start, stop=stop, skip_group_check=True),
                   reads=reads, writes=writes)
        self.note_reads(reads)

    def act(self, out, in_, func, reads, writes, bias=None, scale=None, accum_out=None):
        kw = {}
        if bias is not None:
            kw["bias"] = bias
        if scale is not None:
            kw["scale"] = scale
        if accum_out is not None:
            kw["accum_out"] = accum_out
        self.P.add("act", lambda e: e.activation(out, in_, func, **kw), reads=reads, writes=writes)

    def dve(self, fn, reads, writes):
        self.P.add("dve", fn, reads=reads, writes=writes)

    def dma(self, q, out, in_, semkey, reads, writes):
        self.P.add(q, lambda e: e.dma_start(out=out, in_=in_), reads=reads, writes=writes, dma=semkey)

    NSLOT = 4

    def wq_init(self, rec=None):
        self.wq_dry = rec is None
        self.wq_rec = [] if rec is None else rec[0]
        self.wq_last = [] if rec is None else rec[1]
        self.wq_n = 0
        self.opc = 0
        self.wq_next = 0
        self.wq_slotblk = [None] * self.NSLOT
        self.wq_view = {}

    def _wq_emit(self, b, slot):
        parts, shape = self.wq_rec[b]
        A, C = shape
        sv = self.view("WR", slot * 4096, A * C, dims=[A, C])
        sv.keys = [("w", slot)]
        sv.blk = b
        for (src, a0, c0) in parts:
            cols = src.shape[-1]
            s3 = src.rearrange("(a p) c -> p a c", p=128)
            na = s3.shape[1]
            dst = sv.ap[:, a0:a0 + na, c0:c0 + cols]
            self.dma("pool", dst, s3, ("w", slot), reads=[], writes=[sv])
        self.wq_slotblk[slot] = b
        self.wq_view[b] = sv

    def _wq_free_slot(self):
        for s_ in range(self.NSLOT):
            x = self.wq_slotblk[s_]
            if x is None or (x < self.wq_n and self.opc >= self.wq_last[x]):
                return s_
        return None

    def wq_pump(self):
        if self.wq_dry:
            return
        while self.wq_next < len(self.wq_rec):
            s_ = self._wq_free_slot()
            if s_ is None:
                return
            self._wq_emit(self.wq_next, s_)
            self.wq_next += 1

    def wload(self, parts, shape):
        norm = []
        c0 = 0
        for p_ in parts:
            if isinstance(p_, tuple):
                norm.append(p_)
            else:
                norm.append((p_, 0, c0))
                c0 += p_.shape[-1]
        b = self.wq_n
        self.wq_n += 1
        if self.wq_dry:
            self.wq_rec.append((norm, shape))
            self.wq_last.append(self.opc)
            A, C = shape
            sv = self.view("WR", 0, A * C, dims=[A, C])
            sv.keys = [("w", 0)]
            sv.blk = b
            return sv
        if b >= self.wq_next:
            assert b == self.wq_next
            s_ = self._wq_free_slot()
            assert s_ is not None, "weight ring full at block %d" % b
            self._wq_emit(b, s_)
            self.wq_next += 1
        self.wq_pump()
        return self.wq_view[b]

    def note_reads(self, reads):
        self.opc += 1
        for r in reads:
            if isinstance(r, V) and r.blk is not None:
                if self.wq_dry:
                    self.wq_last[r.blk] = self.opc
        self.wq_pump()


def build(nc):
    B = Builder(nc)
    P = B.P
    dr = B.dr
    B.din("xpT", [D, 1024]); B.din("xsT", [D, 2048])
    B.din("condT", [128, 16]); B.din("bmodT", [128, 2 * 72])
    B.din("lngT", [128, 48]); B.din("lnbT", [128, 48])
    B.din("w_mod", [2, D, 9216])
    for nme in ("ffn1_w1", "ffn1_w3", "ffn2_w1", "ffn2_w3"):
        B.din(nme, [2, D, DFF])
    B.din("ffn1_w2", [2, DFF, D]); B.din("ffn2_w2", [2, DFF, D])
    B.din("w_in", [2, D, DIN]); B.din("w_in_sw", [2, D, 1024])
    B.din("w_pa", [2, 256, D]); B.din("w_pb", [2, 512, D]); B.din("w_pc", [2, 256, D]); B.din("w_out", [2, D, D])
    B.din("poolbd", [2, 2, 128, 128]); B.din("poolsc", [128, 4])
    B.din("Pp", [128, 4 * 2 * 256]); B.din("Ps", [128, 4 * 128])
    B.din("lamv", [128, 2 * 4 * 64]); B.din("sgT", [128, 2])
    B.din("ropeC", [128, 2048]); B.din("ropeS", [128, 2048])
    B.din("nabias", [2, 4, 4, 8, 128, 512])
    B.din("cdkT", [2, 4, 128, 512]); B.din("cdv", [2, 512, 4, 128])
    B.din("cnkT", [2, 2, 128, 512]); B.din("cnv", [2, 512, 4, 64])
    B.din("onesb", [128, 384]); B.din("onesf", [128, 128]); B.din("wsel", [128, 4])
    B.dout("ypT", [D, 1024]); B.dout("ysT", [D, 512])
    B.dout("ndkT", [2, 512, 1024]); B.dout("ndv", [2, 1024, 512])
    B.dout("nnkT", [2, 256, 1024]); B.dout("nnv", [2, 1024, 256])

    return B


def emit_all(B, nc, sb, ps):
    P = B.P
    dr = B.dr
    B.sb = sb
    B.NSCR = 5
    B.NBS = 3
    PS = [V(ps[i], [("ps", i)]) for i in range(7)]
    PS7 = V(ps[7], [("ps", 7)])

    def psv(b, lo=0, n=512, pl=0, ph=128):
        return PS[b].ap[pl:ph, lo:lo + n]

    cst = {}

    def T(name):
        return sb[name][0]

    def cload(name, tname, n, q="sp"):
        v = B.view(tname, 0, n)
        B.dma(q, v.ap, dr[name], ("c", name), reads=[], writes=[v])
        cst[name] = v
        return v

    cload("condT", "condT", 16); cload("bmodT", "bmodT", 144); cload("lngT", "lngT", 48); cload("lnbT", "lnbT", 48)
    cload("poolsc", "poolsc", 4); cload("lamv", "LNS", 512); cload("sgT", "sgT", 2)
    cload("onesf", "onesf", 128)
    cload("wsel", "wsel", 4)
    cload("onesb", "onesb", 384, q="pool")
    cload("Pp", "Pp", 2048, q="pool"); cload("Ps", "Ps", 512, q="pool")
    bdv = B.view("poolbd", 0, 512, dims=[4, 128])
    B.dma("pool", bdv.ap, dr["poolbd"].rearrange("l c p d -> p (l c) d"), ("c", "poolbd"), reads=[], writes=[bdv])
    ones = cst["onesf"]
    epsv = B.view("EPS", 0, 2)
    B.dve(lambda e: e.memset(T("EPS")[:, 0:1], LN_EPS), reads=[], writes=[epsv])
    B.dve(lambda e: e.memset(T("EPS")[:, 1:2], RMS_EPS), reads=[], writes=[epsv])


    scv = B.view("scb", 0, 16, dims=[8, 2])
    B.act(scv.ap, B.view("condT", 0, 16, dims=[8, 2]).ap, AF.Silu, reads=[cst["condT"]], writes=[scv])

    MTv = B.view("MT", 0, 288)
    for l in range(2):
        prev = None
        for blk in range(18):
            W = B.wload([dr["w_mod"][l][:, blk * 512:(blk + 1) * 512]], (8, 512))
            for j in range(4):
                mc = blk * 4 + j
                for k in range(8):
                    B.mm(psv(l, mc * 2, 2), W.ap[:, k, j * 128:(j + 1) * 128], scv.ap[:, k, :], k == 0, k == 7,
                         reads=[W, scv], writes=[PS[l]])
        for ci in range(2):
            o = T("MT")[:, (l * 2 + ci) * 72:(l * 2 + ci + 1) * 72]
            i0 = PS[l].ap[:, 0:144].rearrange("p (m c) -> p m c", c=2)[:, :, ci]
            i1 = T("bmodT")[:, l * 72:(l + 1) * 72]
            B.dve(lambda e, o=o, i0=i0, i1=i1: e.tensor_tensor(o, i0, i1, ALU.add),
                  reads=[PS[l], cst["bmodT"]], writes=[MTv])
    Av = B.view("MA", 0, 96); Gv = B.view("MG", 0, 96)

    def mcol(l, ci, v, c):
        o = (l * 2 + ci) * 72 + v * 8 + c
        return T("MT")[:, o:o + 1]

    for l in range(2):
        for ci in range(2):
            for i in range(3):
                base = (l * 2 + ci) * 72
                o = ((l * 2 + ci) * 3 + i) * 8
                sc_in = T("MT")[:, base + (3 * i + 1) * 8: base + (3 * i + 2) * 8]
                g_in = T("MT")[:, base + (3 * i + 2) * 8: base + (3 * i + 3) * 8]
                B.dve(lambda e, o=o, a=sc_in: e.tensor_scalar(T("MA")[:, o:o + 8], a, 1.0, 1.0 / ALPHA, ALU.add, ALU.mult),
                      reads=[MTv], writes=[Av])
                gm = 1.0 if i == 1 else 0.5
                B.dve(lambda e, o=o, a=g_in, gm=gm: e.tensor_scalar(T("MG")[:, o:o + 8], a, gm, None, ALU.mult),
                      reads=[MTv], writes=[Gv])
    AGv = B.view("AG", 0, 48); ABv = B.view("AB", 0, 48)
    B.dve(lambda e: e.tensor_scalar(T("AG")[:, 0:48], T("lngT")[:, 0:48], ALPHA, None, ALU.mult), reads=[cst["lngT"]], writes=[AGv])
    B.dve(lambda e: e.tensor_scalar(T("AB")[:, 0:48], T("lnbT")[:, 0:48], ALPHA, None, ALU.mult), reads=[cst["lnbT"]], writes=[ABv])

    def A_(l, ci, i, c):
        o = ((l * 2 + ci) * 3 + i) * 8 + c
        return T("MA")[:, o:o + 1]

    def G_(l, ci, i, c):
        o = ((l * 2 + ci) * 3 + i) * 8 + c
        return T("MG")[:, o:o + 1]

    def SH_(l, ci, i, c):
        return mcol(l, ci, 3 * i, c)

    lamv = cst["lamv"]
    LMv = B.view("LM", 0, 16)
    for l in range(2):
        lam_init = 0.8 - 0.6 * math.exp(-0.3 * l)
        for j in range(2):
            a = T("LNS")[:, (l * 4 + 2 * j) * 64:(l * 4 + 2 * j + 1) * 64]
            b = T("LNS")[:, (l * 4 + 2 * j + 1) * 64:(l * 4 + 2 * j + 2) * 64]
            s = B.scratch(64)
            B.dve(lambda e, s=s, a=a, b=b: e.tensor_tensor(s.ap, a, b, ALU.mult), reads=[lamv], writes=[s])
            o = T("LM")[:, l * 8 + j: l * 8 + j + 1]
            B.dve(lambda e, s=s, o=o: e.tensor_reduce(o, s.ap, mybir.AxisListType.X, ALU.add), reads=[s], writes=[LMv])
            o2 = T("LM")[:, l * 8 + 2 + j: l * 8 + 3 + j]
            B.act(o2, o, AF.Exp, reads=[LMv], writes=[LMv])
        e1 = T("LM")[:, l * 8 + 2:l * 8 + 3]; e2 = T("LM")[:, l * 8 + 3:l * 8 + 4]; d_ = T("LM")[:, l * 8 + 4:l * 8 + 5]
        nl = T("LM")[:, l * 8 + 5:l * 8 + 6]
        B.dve(lambda e, e1=e1, e2=e2, d_=d_: e.tensor_tensor(d_, e2, e1, ALU.subtract), reads=[LMv], writes=[LMv])
        B.dve(lambda e, d_=d_, nl=nl, li=lam_init: e.tensor_scalar(nl, d_, -li, None, ALU.add), reads=[LMv], writes=[LMv])
        sg = T("sgT")[:, l:l + 1]
        B.dve(lambda e, sg=sg, li=lam_init: e.tensor_scalar(sg, sg, 1.0 - li, None, ALU.mult), reads=[lamv, cst["sgT"]], writes=[cst["sgT"]])

    def NEGLAM(l):
        return T("LM")[:, l * 8 + 5:l * 8 + 6]

    pname = ["-"]
    P.phase = "mod"

    toff = [0]

    def xa(c, tt, n=512):
        return B.view("XA", c * 2048 + (tt + toff[0]) * 512, n)

    def xa_half(hi):
        off = 1024 * hi
        full = B.view("XA", 0, 16384, dims=[8, 2048])
        keys = []
        for c_ in range(8):
            keys += B.view("XA", c_ * 2048 + off, 1024).keys
        return V(full.ap[:, :, off:off + 1024], keys)

    pend_ln = {}

    def ln_later(l, i, tt, final):
        pend_ln[tt] = (l, i, final)

    def alpha_scale(tt):
        for c in range(8):
            x = xa(c, tt)
            B.act(x.ap, x.ap, AF.Copy, reads=[x], writes=[x], scale=ALPHA)

    def ensure_ln(tt):
        if tt in pend_ln:
            ent = pend_ln.pop(tt)
            if ent[0] == "scale":
                alpha_scale(tt)
            else:
                l_, i_, f_ = ent
                ln_tile(l_, i_, tt, f_)

    def ln_tile(l, i, tt, final):
        old_phase = P.phase
        P.phase = P.phase.split("|")[0] + "|ln"
        S1, S2 = PS[5], PS[6]
        for c in range(8):
            x = xa(c, tt)
            sq = B.scratch()
            B.act(sq.ap, x.ap, AF.Square, reads=[x], writes=[sq])
            B.mm(psv(5), ones.ap, x.ap, c == 0, c == 7, reads=[ones, x], writes=[S1])
            B.mm(psv(6), ones.ap, sq.ap, c == 0, c == 7, reads=[ones, sq], writes=[S2])
        mu = B.view("LNS", 0, 512); rstd = B.view("LNS", 512, 512); mr = B.view("LNS", 1024, 512)
        B.dve(lambda e: e.tensor_copy(mu.ap, psv(5)), reads=[S1], writes=[mu])
        B.dve(lambda e: e.tensor_tensor(mr.ap, mu.ap, mu.ap, ALU.mult), reads=[mu], writes=[mr])
        B.dve(lambda e: e.tensor_tensor(rstd.ap, psv(6), mr.ap, ALU.subtract), reads=[S2, mr], writes=[rstd])
        B.act(rstd.ap, rstd.ap, AF.Ln, reads=[rstd, epsv], writes=[rstd], bias=T("EPS")[:, 0:1])
        B.act(rstd.ap, rstd.ap, AF.Exp, reads=[rstd], writes=[rstd], scale=-0.5)
        B.dve(lambda e: e.tensor_tensor(mr.ap, mu.ap, rstd.ap, ALU.mult), reads=[mu, rstd], writes=[mr])
        t1s = {}

        def stage2(c):
            x = xa(c, tt)
            t1 = t1s.pop(c)
            B.dve(lambda e, t1=t1: e.tensor_tensor(t1.ap, t1.ap, mr.ap, ALU.subtract), reads=[t1, mr], writes=[t1])
            o = (l * 3 + i) * 8 + c
            if final:
                sc = T("lngT")[:, o:o + 1]; bi = T("lnbT")[:, o:o + 1]
                rd = [cst["lngT"], cst["lnbT"]]
            else:
                sc = T("AG")[:, o:o + 1]; bi = T("AB")[:, o:o + 1]
                rd = [AGv, ABv]
            B.act(x.ap, t1.ap, AF.Identity, reads=[t1] + rd, writes=[x], bias=bi, scale=sc)

        for c in range(8):
            x = xa(c, tt)
            t1 = B.scratch()
            t1s[c] = t1
            B.dve(lambda e, x=x, t1=t1: e.tensor_tensor(t1.ap, x.ap, rstd.ap, ALU.mult), reads=[x, rstd], writes=[t1])
            if c >= 1:
                stage2(c - 1)
        stage2(7)
        P.phase = old_phase

    def modx_into(dst, l, ci, i, c, tt):
        ensure_ln(tt)
        x = xa(c, tt)
        B.act(dst.ap, x.ap, AF.Identity, reads=[x, Av, MTv], writes=[dst], bias=SH_(l, ci, i, c), scale=A_(l, ci, i, c))

    def ffn(l, which, NT, ci, final, tgroups=None):
        i = 0 if which == 1 else 2
        if tgroups is None:
            tgroups = [[2 * hf, 2 * hf + 1] for hf in range(NT // 2)]
        w1 = dr["ffn%d_w1" % which][l]; w3 = dr["ffn%d_w3" % which][l]; w2 = dr["ffn%d_w2" % which][l]
        pair_i = 0
        bank_i = 0
        for tiles in tgroups:

            def mx(c, tl):
                return B.view("G", 11264 + c * 1024 + tl * 512, 512)

            def gg(fl, tl):
                return B.view("G", fl * 1024 + tl * 512, 512)

            for tl, tt in enumerate(tiles):
                for c in range(8):
                    modx_into(mx(c, tl), l, ci, i, c, tt)
            for ffh in range(2):
                P.phase = "%s.ffn%d|ph1" % (pname[0], which)
                blocks = [(0, 4), (4, 4), (8, 3)]
                loaded = {}

                def ld(bi):
                    f0, nf = blocks[bi]
                    cs = (ffh * 11 + f0) * 128
                    a = B.wload([w1[:, cs:cs + nf * 128]], (8, nf * 128))
                    b = B.wload([w3[:, cs:cs + nf * 128]], (8, nf * 128))
                    return a, b

                for bi in range(3):
                    for j in range(bi, bi + 1):
                        if j not in loaded:
                            loaded[j] = ld(j)
                    W1b, W3b = loaded[bi]
                    f0, nf = blocks[bi]
                    for tl, tt in enumerate(tiles):
                        for j in range(nf):
                            fl = f0 + j
                            b1 = (pair_i % 2) * 2; b3 = b1 + 1; pair_i += 1
                            for k in range(8):
                                m = mx(k, tl)
                                B.mm(psv(b1), W1b.ap[:, k, j * 128:(j + 1) * 128], m.ap, k == 0, k == 7, reads=[W1b, m], writes=[PS[b1]])
                            for k in range(8):
                                m = mx(k, tl)
                                B.mm(psv(b3), W3b.ap[:, k, j * 128:(j + 1) * 128], m.ap, k == 0, k == 7, reads=[W3b, m], writes=[PS[b3]])
                            s = B.scratch()
                            B.act(s.ap, psv(b1), AF.Silu, reads=[PS[b1]], writes=[s])
                            g = gg(fl, tl)
                            B.dve(lambda e, g=g, s=s, b3=b3: e.tensor_tensor(g.ap, s.ap, psv(b3), ALU.mult), reads=[s, PS[b3]], writes=[g])
                P.phase = "%s.ffn%d|ph2" % (pname[0], which)
                loaded = {}

                def ld2(dp):
                    r0 = ffh * 1408
                    return B.wload([w2[r0:r0 + 1408, dp * 256:(dp + 1) * 256]], (11, 256))

                for dp in range(4):
                    for j in range(dp, dp + 1):
                        if j not in loaded:
                            loaded[j] = ld2(j)
                    W2b = loaded[dp]
                    for dd in range(2):
                        d = dp * 2 + dd
                        for tl, tt in enumerate(tiles):
                            b = bank_i % 5; bank_i += 1
                            for fl in range(11):
                                g = gg(fl, tl)
                                B.mm(psv(b), W2b.ap[:, fl, dd * 128:(dd + 1) * 128], g.ap, fl == 0, fl == 10, reads=[W2b, g], writes=[PS[b]])
                            x = xa(d, tt)
                            B.dve(lambda e, x=x, b=b, d=d: e.scalar_tensor_tensor(x.ap, psv(b), G_(l, ci, i, d), x.ap, ALU.mult, ALU.add),
                                  reads=[PS[b], x, Gv], writes=[x])
            for tt in tiles:
                ln_later(l, i, tt, final)

    def mixer(l, NT, ci, sample, own=False):
        ntok = NT * 512
        nblk = NT * 4

        def yT(ch, tt):
            return B.view("G", ch * ntok + tt * 512, 512)

        def qT(pl=0, ph=128, c0=0, n=512):
            v = B.view("H", c0, n, pl=pl, ph=ph)
            return v

        def kT(pl=0, ph=128, c0=0, n=512):
            return B.view("H", 2048 + c0, n, pl=pl, ph=ph)

        def vaug(tb, c0=0, n=256):
            return B.view("H", 4608 + tb * 256 + c0, n)

        def aTM(tb):
            return B.view("H", 7424 + tb * 256, 256)

        ering = [0]

        def Ebuf():
            i_ = ering[0]; ering[0] = (i_ + 1) % 4
            return B.view("H", 11520 + i_ * 512, 512)

        if not sample:
            def mxv(c, tt):
                return B.view("G", 8 * ntok + c * ntok + tt * 512, 512)
            for tt in range(NT):
                for c in range(8):
                    modx_into(mxv(c, tt), l, ci, 1, c, tt)

            def get_modx(tt):
                return [mxv(c, tt) for c in range(8)]
        else:
            def get_modx(tt, buf=0):
                base = 16384 if buf == 0 else 4 * ntok
                vs = [B.view("G", base + c * 512, 512) for c in range(8)]
                for c in range(8):
                    modx_into(vs[c], l, ci, 1, c, tt)
                return vs

        win = dr["w_in"][l]
        bank = [0]
        wsel = cst["wsel"]

        def select_tiles(vf, ntl):
            v0 = vf(0)
            B.dve(lambda e, v0=v0: e.tensor_scalar(v0.ap, v0.ap, T("wsel")[:, 0:1], None, ALU.mult), reads=[v0, wsel], writes=[v0])
            for t_ in range(1, ntl):
                vt_ = vf(t_)
                B.dve(lambda e, v0=v0, vt_=vt_, t_=t_: e.scalar_tensor_tensor(v0.ap, vt_.ap, T("wsel")[:, t_:t_ + 1], v0.ap, ALU.mult, ALU.add),
                      reads=[v0, vt_, wsel], writes=[v0])

        def nb(n=5):
            b = bank[0] % n; bank[0] = (b + 1) % n
            return b

        P.phase = "%s.mix|pool" % pname[0]
        Wa = B.wload([win[:, 0:256]], (8, 256))
        for tt in range(NT):
            mxs = get_modx(tt)
            for tb4 in range(4):
                tb = tt * 4 + tb4
                b = nb()
                for k in range(8):
                    B.mm(psv(b, 0, 256), mxs[k].ap[:, tb4 * 128:(tb4 + 1) * 128], Wa.ap[:, k, :], k == 0, k == 7, reads=[mxs[k], Wa], writes=[PS[b]])
                a_ = aTM(tb)
                B.dve(lambda e, a_=a_, b=b: e.tensor_copy(a_.ap, psv(b, 0, 256)), reads=[PS[b]], writes=[a_])
        Ppv = cst["Pp"]; Psv = cst["Ps"]
        for tt in range(NT):
            for cch in range(2):
                b = nb()
                if sample:
                    for tb4 in range(4):
                        tb = tt * 4 + tb4
                        for g2 in range(2):
                            g = cch * 2 + g2
                            a_ = aTM(tb)
                            B.mm(psv(b, tb4 * 128, 128, g2 * 64, g2 * 64 + 64), a_.ap[:, g * 64:(g + 1) * 64],
                                 T("Ps")[:, g * 128:(g + 1) * 128], True, True, reads=[a_, Psv], writes=[PS[b]])
                else:
                    for sq in range(2):
                        seq = tt * 2 + sq
                        for g2 in range(2):
                            g = cch * 2 + g2
                            for sblk in range(2):
                                a_ = aTM(seq * 2 + sblk)
                                B.mm(psv(b, sq * 256, 256, g2 * 64, g2 * 64 + 64), a_.ap[:, g * 64:(g + 1) * 64],
                                     T("Pp")[:, (g * 2 + sblk) * 256:(g * 2 + sblk + 1) * 256], sblk == 0, sblk == 1,
                                     reads=[a_, Ppv], writes=[PS[b]])
                pT = B.bscratch()
                B.dve(lambda e, pT=pT, b=b: e.tensor_copy(pT.ap, psv(b)), reads=[PS[b]], writes=[pT])
                b2 = nb()
                B.mm(psv(b2), bdv.ap[:, l * 2 + cch, :], pT.ap, True, True, reads=[bdv, pT], writes=[PS[b2]])
                y = yT(6 + cch, tt)
                B.act(y.ap, psv(b2), AF.Identity, reads=[PS[b2], cst["poolsc"]], writes=[y], bias=0.0, scale=T("poolsc")[:, l * 2 + cch:l * 2 + cch + 1])

        class Step:
            __slots__ = ("pl", "ph", "kc0", "vt", "bias", "vc0", "oc0", "accO", "accS", "first", "last")

        def mkstep(**kw):
            st = Step()
            for k_, v_ in kw.items():
                setattr(st, k_, v_)
            return st

        def run_groups(groups, LA=2):
            flat = []
            for (nq, qc0, steps, post) in groups:
                for si, st in enumerate(steps):
                    flat.append((nq, qc0, st, post if si == len(steps) - 1 else None))
            pend = []
            issued_b = set()
            bcache = {}

            def do_score(nq, qc0, st):
                sb_ = nb(3)
                q_ = B.view("H", (0 if st.pl == 0 else 13568) + qc0, nq); k_ = kT(0, 128, st.kc0, 128)
                B.mm(psv(sb_, 0, nq), k_.ap, q_.ap, True, True, reads=[k_, q_], writes=[PS[sb_]])
                E = Ebuf()
                bias = bcache.pop(id(st), None)
                if bias is None:
                    B.act(E.ap[:, 0:nq], psv(sb_, 0, nq), AF.Exp, reads=[PS[sb_]], writes=[E], scale=0.125)
                else:
                    t = B.scratch()
                    B.dve(lambda e, t=t, sb_=sb_, bias=bias: e.scalar_tensor_tensor(t.ap, psv(sb_), 0.125, bias.ap, ALU.mult, ALU.add),
                          reads=[PS[sb_], bias], writes=[t])
                    B.act(E.ap, t.ap, AF.Exp, reads=[t], writes=[E])
                return E

            def do_pv(nq, st, E):
                vv = vaug(st.vt)
                B.mm(PS[st.accO].ap[:, 0:nq], vv.ap[:, st.vc0:st.vc0 + 128], E.ap[:, 0:nq], st.first, st.last,
                     reads=[E, vv, ("vcg",)], writes=[PS[st.accO]])
                B.mm(PS[st.accS].ap[:, 0:nq], T("onesb")[:, st.oc0:st.oc0 + 128], E.ap[:, 0:nq], st.first, st.last,
                     reads=[E, cst["onesb"]], writes=[PS[st.accS]])

            n = len(flat)
            for i in range(n + LA):
                for j_ in range(i, min(i + 3, n)):
                    stj = flat[j_][2]
                    if stj.bias is not None and j_ not in issued_b:
                        bcache[id(stj)] = stj.bias()
                        issued_b.add(j_)
                if i < n:
                    nq, qc0, st, _p = flat[i]
                    pend.append(do_score(nq, qc0, st))
                if i >= LA:
                    nq, qc0, st, post = flat[i - LA]
                    do_pv(nq, st, pend[i - LA])
                    if post is not None:
                        post()

        pending = [None]

        def flush_pending():
            if pending[0] is not None:
                pending[0]()
                pending[0] = None

        def make_finish(ytm, nq, ych, qc0):
            def fin():
                for qb in range(nq // 128):
                    B.P.add("pe", lambda e, qb=qb: e.transpose(PS7.ap[:, qb * 128:(qb + 1) * 128], ytm.ap[:, qb * 128:(qb + 1) * 128], ident.ap),
                            reads=[ytm, ident], writes=[PS7])
                yv = B.view("G", ych * ntok + qc0, nq)
                B.act(yv.ap, PS7.ap[:, 0:nq], AF.Copy, reads=[PS7], writes=[yv])
            return fin


        def qsplit(tt_):
            a_hi = B.view("H", tt_ * 512, 512, pl=64, ph=128)
            b_hi = B.view("H", 13568 + tt_ * 512, 512, pl=64, ph=128)
            B.act(b_hi.ap, a_hi.ap, AF.Copy, reads=[a_hi], writes=[b_hi])
            B.dve(lambda e, a_hi=a_hi: e.memset(a_hi.ap, 0.0), reads=[], writes=[a_hi])

        def qzero():
            b_lo = B.view("H", 13568, ntok, pl=0, ph=64)
            B.dve(lambda e, b_lo=b_lo: e.memset(b_lo.ap, 0.0), reads=[], writes=[b_lo])

        rope_tabs = {}
        mx_cache = {}
        if not sample:
            for tt in range(NT):
                mx_cache[tt] = get_modx(tt)

        nkt_lat = nblk
        for h in range(4):
            P.phase = "%s.mix|dproj" % pname[0]
            W1_ = B.wload([win[:, 256 + 128 * h:384 + 128 * h], win[:, 768 + 128 * h:896 + 128 * h],
                                        win[:, 1280 + 128 * h:1408 + 128 * h]], (8, 384))
            Wsw = None
            if sample:
                wsw = dr["w_in_sw"][l]
                Wsw = B.wload([wsw[:, 128 * h:128 * h + 128], wsw[:, 512 + 128 * h:640 + 128 * h]], (8, 256))
            qzero()
            for tt in range(NT):
                if sample:
                    if tt == 0:
                        mx_cache[0] = get_modx(0, 0)
                    if tt + 1 < NT:
                        mx_cache[tt + 1] = get_modx(tt + 1, (tt + 1) % 2)
                    Ct = B.view("FR", 0, 512); St = B.view("FR", 512, 512)
                    B.dma("sp", Ct.ap, dr["ropeC"][:, tt * 512:(tt + 1) * 512], ("fr", 0), reads=[], writes=[Ct])
                    B.dma("sp", St.ap, dr["ropeS"][:, tt * 512:(tt + 1) * 512], ("fr", 1), reads=[], writes=[St])
                    rope_tabs[tt] = (Ct, St)
                mxs = mx_cache[tt]
                for (wc0, dstf, rp, od) in (
                    (0, lambda t_: qT(0, 128, t_ * 512, 512), (Wsw, 0) if sample else None, None),
                    (128, lambda t_: kT(0, 128, t_ * 512, 512), (Wsw, 128) if sample else None,
                     (lambda t_: dr["ndkT"][l][h * 128:(h + 1) * 128, t_ * 512:(t_ + 1) * 512]) if not sample else None),
                ):
                    b = nb()
                    for k in range(8):
                        B.mm(psv(b), W1_.ap[:, k, wc0:wc0 + 128], mxs[k].ap, k == 0, k == 7, reads=[W1_, mxs[k]], writes=[PS[b]])
                    dst = dstf(tt)
                    if rp is None and od is not None:
                        s = B.scratch()
                        B.dve(lambda e, s=s, b=b: e.tensor_copy(s.ap, psv(b)), reads=[PS[b]], writes=[s])
                        B.act(dst.ap, s.ap, AF.Copy, reads=[s], writes=[dst])
                        B.dma("sp", od(tt), s.ap, ("scr", s.keys[0][1]), reads=[s], writes=[("out", "kv")])
                    elif rp is None:
                        B.act(dst.ap, psv(b), AF.Copy, reads=[PS[b]], writes=[dst])
                    else:
                        Wsw_, sc0 = rp
                        b2 = nb()
                        for k in range(8):
                            B.mm(psv(b2), Wsw_.ap[:, k, sc0:sc0 + 128], mxs[k].ap, k == 0, k == 7, reads=[Wsw_, mxs[k]], writes=[PS[b2]])
                        Ct, St = rope_tabs[tt]
                        t = B.scratch(); u = B.scratch()
                        B.dve(lambda e, t=t, b=b, Ct=Ct: e.tensor_tensor(t.ap, psv(b), Ct.ap, ALU.mult), reads=[PS[b], Ct], writes=[t])
                        B.dve(lambda e, u=u, b2=b2, St=St: e.tensor_tensor(u.ap, psv(b2), St.ap, ALU.mult), reads=[PS[b2], St], writes=[u])
                        B.dve(lambda e, t=t, u=u, dst=dst: e.tensor_tensor(dst.ap, t.ap, u.ap, ALU.add), reads=[t, u], writes=[dst])
                    if wc0 == 0:
                        qsplit(tt)
                for tb4 in range(4):
                    tb = tt * 4 + tb4
                    b = nb()
                    for k in range(8):
                        B.mm(psv(b, 0, 128), mxs[k].ap[:, tb4 * 128:(tb4 + 1) * 128], W1_.ap[:, k, 256:384], k == 0, k == 7,
                             reads=[mxs[k], W1_], writes=[PS[b]])
                    vv = vaug(tb)
                    if not sample:
                        s = B.scratch(128)
                        B.act(s.ap, psv(b, 0, 128), AF.Copy, reads=[PS[b]], writes=[s])
                        B.dve(lambda e, vv=vv, s=s: e.tensor_copy(vv.ap[:, 0:128], s.ap), reads=[s], writes=[vv])
                        B.dma("sp", dr["ndv"][l][tb * 128:(tb + 1) * 128, h * 128:(h + 1) * 128], s.ap, ("scr", s.keys[0][1]),
                              reads=[s], writes=[("out", "kv")])
                    else:
                        B.dve(lambda e, vv=vv, b=b: e.tensor_copy(vv.ap[:, 0:128], psv(b, 0, 128)), reads=[PS[b]], writes=[vv])
            if sample:
                kc = kT(0, 128, 2048, 512)
                s1 = B.scratch()
                B.dma("sp", s1.ap, dr["cdkT"][l][h], ("scr", s1.keys[0][1]), reads=[], writes=[s1])
                B.act(kc.ap, s1.ap, AF.Copy, reads=[s1], writes=[kc])
                s2 = B.scratch()
                B.dma("sp", s2.ap.rearrange("p (k d) -> p k d", k=4), dr["cdv"][l][:, h, :].rearrange("(k p) d -> p k d", p=128),
                      ("scr", s2.keys[0][1]), reads=[], writes=[s2])
                vall = B.view("H", 4608 + 16 * 256, 4 * 256, dims=[4, 256])
                B.dve(lambda e, vall=vall, s2=s2: e.tensor_copy(vall.ap[:, :, 0:128], s2.ap.rearrange("p (k d) -> p k d", k=4)),
                      reads=[s2], writes=[vall, ("vcg",)])
            P.phase = "%s.mix|dattn" % pname[0]
            if own:
                select_tiles(lambda t_: qT(0, 128, t_ * 512, 512), NT)
                select_tiles(lambda t_: B.view("H", 13568 + t_ * 512, 512), NT)
            if sample:
                groups = [(512, qt * 512, [(kt * 128, kt, None) for kt in range(20)]) for qt in range(1 if own else 4)]
            else:
                groups = [(256, sq * 256, [((2 * sq + j) * 128, 2 * sq + j, None) for j in range(2)]) for sq in range(4)]
            glist = []
            for (nq, qc0, ktl) in groups:
                steps = []
                for (pl, aO, aS) in ((0, 3, 4), (64, 5, 6)):
                    for ki, (kc0, vt, _b) in enumerate(ktl):
                        steps.append(mkstep(pl=pl, ph=pl + 64, kc0=kc0, vt=vt, bias=None, vc0=0, oc0=0, accO=aO, accS=aS,
                                            first=(ki == 0), last=(ki == len(ktl) - 1)))

                def post(nq=nq, qc0=qc0):
                    r0 = B.scratch(); r1 = B.scratch(); t = B.scratch(); o_ = B.scratch()
                    B.act(r0.ap[:, 0:nq], PS[4].ap[:, 0:nq], AF.Ln, reads=[PS[4]], writes=[r0])
                    B.act(r1.ap[:, 0:nq], PS[6].ap[:, 0:nq], AF.Ln, reads=[PS[6]], writes=[r1])
                    B.act(r0.ap[:, 0:nq], r0.ap[:, 0:nq], AF.Exp, reads=[r0], writes=[r0], scale=-1.0)
                    B.act(r1.ap[:, 0:nq], r1.ap[:, 0:nq], AF.Exp, reads=[r1], writes=[r1], scale=-1.0)
                    B.dve(lambda e, t=t, r1=r1: e.scalar_tensor_tensor(t.ap[:, 0:nq], PS[5].ap[:, 0:nq], NEGLAM(l), r1.ap[:, 0:nq], ALU.mult, ALU.mult),
                          reads=[PS[5], r1, LMv], writes=[t])
                    B.dve(lambda e, o_=o_, r0=r0: e.tensor_tensor(o_.ap[:, 0:nq], PS[3].ap[:, 0:nq], r0.ap[:, 0:nq], ALU.mult),
                          reads=[PS[3], r0], writes=[o_])
                    B.dve(lambda e, o_=o_, t=t: e.tensor_tensor(o_.ap[:, 0:nq], o_.ap[:, 0:nq], t.ap[:, 0:nq], ALU.add), reads=[o_, t], writes=[o_])
                    B.act(r1.ap[:, 0:nq], o_.ap[:, 0:nq], AF.Square, reads=[o_], writes=[r1])
                    mb = nb(3)
                    B.mm(psv(mb, 0, nq), ones.ap, r1.ap[:, 0:nq], True, True, reads=[ones, r1], writes=[PS[mb]])
                    B.act(r0.ap[:, 0:nq], psv(mb, 0, nq), AF.Ln, reads=[PS[mb], epsv], writes=[r0], bias=T("EPS")[:, 1:2], scale=8.0)
                    B.act(r0.ap[:, 0:nq], r0.ap[:, 0:nq], AF.Exp, reads=[r0], writes=[r0], scale=-0.5)
                    yv = B.view("G", h * ntok + qc0, nq)
                    B.dve(lambda e, yv=yv, o_=o_, r0=r0: e.scalar_tensor_tensor(yv.ap, o_.ap[:, 0:nq], T("sgT")[:, l:l + 1], r0.ap[:, 0:nq], ALU.mult, ALU.mult),
                          reads=[o_, r0, cst["sgT"]], writes=[yv])
                glist.append((nq, qc0, steps, post))
            run_groups(glist)

        for cch in range(2):
            P.phase = "%s.mix|nproj" % pname[0]
            W1_ = B.wload([win[:, 1792 + 128 * cch:1920 + 128 * cch], win[:, 2048 + 128 * cch:2176 + 128 * cch],
                                          win[:, 2304 + 128 * cch:2432 + 128 * cch]], (8, 384))
            qzero()
            for tb in range(20 if sample else nblk):
                vv = vaug(tb)
                B.dve(lambda e, vv=vv: e.memset(vv.ap[:, 64:192], 0.0), reads=[], writes=[vv])
            for tt in range(NT):
                if sample:
                    mx_cache[tt] = get_modx(tt)
                mxs = mx_cache[tt]
                for (wc0, dstf, od) in (
                    (0, lambda t_: qT(0, 128, t_ * 512, 512), None),
                    (128, lambda t_: kT(0, 128, t_ * 512, 512),
                     (lambda t_: dr["nnkT"][l][cch * 128:(cch + 1) * 128, t_ * 512:(t_ + 1) * 512]) if not sample else None),
                ):
                    b = nb()
                    for k in range(8):
                        B.mm(psv(b), W1_.ap[:, k, wc0:wc0 + 128], mxs[k].ap, k == 0, k == 7, reads=[W1_, mxs[k]], writes=[PS[b]])
                    dst = dstf(tt)
                    if od is not None:
                        s = B.scratch()
                        B.dve(lambda e, s=s, b=b: e.tensor_copy(s.ap, psv(b)), reads=[PS[b]], writes=[s])
                        B.act(dst.ap, s.ap, AF.Copy, reads=[s], writes=[dst])
                        B.dma("sp", od(tt), s.ap, ("scr", s.keys[0][1]), reads=[s], writes=[("out", "kv")])
                    else:
                        B.act(dst.ap, psv(b), AF.Copy, reads=[PS[b]], writes=[dst])
                    if wc0 == 0:
                        qsplit(tt)
                for tb4 in range(4):
                    tb = tt * 4 + tb4
                    b = nb()
                    for k in range(8):
                        B.mm(psv(b, 0, 128), mxs[k].ap[:, tb4 * 128:(tb4 + 1) * 128], W1_.ap[:, k, 256:384], k == 0, k == 7,
                             reads=[mxs[k], W1_], writes=[PS[b]])
                    vv = vaug(tb)
                    if not sample:
                        s = B.scratch(128)
                        B.act(s.ap, psv(b, 0, 128), AF.Copy, reads=[PS[b]], writes=[s])
                        B.dve(lambda e, vv=vv, s=s: e.tensor_copy(vv.ap[:, 0:64], s.ap[:, 0:64]), reads=[s], writes=[vv])
                        B.dve(lambda e, vv=vv, s=s: e.tensor_copy(vv.ap[:, 192:256], s.ap[:, 64:128]), reads=[s], writes=[vv])
                        B.dma("sp", dr["nnv"][l][tb * 128:(tb + 1) * 128, cch * 128:(cch + 1) * 128], s.ap, ("scr", s.keys[0][1]),
                              reads=[s], writes=[("out", "kv")])
                    else:
                        B.dve(lambda e, vv=vv, b=b: e.tensor_copy(vv.ap[:, 0:64], psv(b, 0, 64)), reads=[PS[b]], writes=[vv])
                        B.dve(lambda e, vv=vv, b=b: e.tensor_copy(vv.ap[:, 192:256], psv(b, 64, 64)), reads=[PS[b]], writes=[vv])
            if sample:
                kc = kT(0, 128, 2048, 512)
                s1 = B.scratch()
                B.dma("sp", s1.ap, dr["cnkT"][l][cch], ("scr", s1.keys[0][1]), reads=[], writes=[s1])
                B.act(kc.ap, s1.ap, AF.Copy, reads=[s1], writes=[kc])
                s2 = B.scratch()
                B.dma("sp", s2.ap.rearrange("p (k h d) -> p k h d", k=4, h=2),
                      dr["cnv"][l][:, 2 * cch:2 * cch + 2, :].rearrange("(k p) h d -> p k h d", p=128),
                      ("scr", s2.keys[0][1]), reads=[], writes=[s2])
                vall = B.view("H", 4608 + 16 * 256, 4 * 256, dims=[4, 256])
                s2v = s2.ap.rearrange("p (k h d) -> p k h d", k=4, h=2)
                B.dve(lambda e, vall=vall, s2v=s2v: e.tensor_copy(vall.ap[:, :, 0:64], s2v[:, :, 0, :]), reads=[s2], writes=[vall, ("vcg",)])
                B.dve(lambda e, vall=vall, s2v=s2v: e.tensor_copy(vall.ap[:, :, 192:256], s2v[:, :, 1, :]), reads=[s2], writes=[vall, ("vcg",)])
            biasctr = [0]
            P.phase = "%s.mix|nattn" % pname[0]
            glist = []
            for gi in range(4):
                if sample:
                    nq = 512; qc0 = gi * 512
                    lo = max(8 * gi - 4, 0); hi = min(8 * gi + 12, 32)
                else:
                    nq = 256; qc0 = gi * 256
                steps = []
                for hh in range(2):
                    head = 2 * cch + hh
                    if sample:
                        ktl = [((kr // 2) * 128, kr // 2, j) for j, kr in enumerate(range(lo, hi, 2))]
                        ktl += [(2048 + kt * 128, 16 + kt, None) for kt in range(4)]
                    else:
                        ktl = [((2 * gi + j) * 128, 2 * gi + j, None) for j in range(2)]
                    for ki, (kc0, vt, j) in enumerate(ktl):
                        bias = None
                        if j is not None:
                            def bias(head=head, gi=gi, j=j):
                                slot = biasctr[0] % 4; biasctr[0] += 1
                                bt = B.view("FR", slot * 512, 512)
                                B.dma("sp", bt.ap, dr["nabias"][l][head][gi][j], ("fr", slot), reads=[], writes=[bt])
                                return bt
                        steps.append(mkstep(pl=hh * 64, ph=hh * 64 + 64, kc0=kc0, vt=vt, bias=bias, vc0=hh * 128, oc0=128 + hh * 128,
                                            accO=3, accS=4, first=(hh == 0 and ki == 0), last=(hh == 1 and ki == len(ktl) - 1)))

                def post(nq=nq, qc0=qc0):
                    r = B.scratch()
                    B.act(r.ap[:, 0:nq], PS[4].ap[:, 0:nq], AF.Ln, reads=[PS[4]], writes=[r])
                    B.act(r.ap[:, 0:nq], r.ap[:, 0:nq], AF.Exp, reads=[r], writes=[r], scale=-1.0)
                    yv = B.view("G", (4 + cch) * ntok + qc0, nq)
                    B.dve(lambda e, yv=yv, r=r: e.tensor_tensor(yv.ap, PS[3].ap[:, 0:nq], r.ap[:, 0:nq], ALU.mult), reads=[PS[3], r], writes=[yv])
                glist.append((nq, qc0, steps, post))
            run_groups(glist)

        if own:
            P.phase = "%s.mix|select" % pname[0]
            for ch in range(4, 8):
                select_tiles(lambda t_, ch=ch: yT(ch, t_), NT)
            for t_ in range(NT):
                ensure_ln(t_)
            for c in range(8):
                select_tiles(lambda t_, c=c: xa(c, t_), NT)
        for tiles in ([[0]] if own else [[2 * hf, 2 * hf + 1] for hf in range(NT // 2)]):
            P.phase = "%s.mix|merge" % pname[0]

            def mxh(c, tl):
                return B.view("H", 8192 + c * 1024 + tl * 512, 512)

            def mixed(c, tl):
                return B.view("H", c * 1024 + tl * 512, 512)

            for tl, tt in enumerate(tiles):
                for c in range(8):
                    modx_into(mxh(c, tl), l, ci, 1, c, tt)
            Wpac = B.wload([(dr["w_pa"][l], 0, 0), (dr["w_pc"][l], 2, 0)], (4, 1024))
            Wpb = B.wload([dr["w_pb"][l]], (4, 1024))
            loaded = {}

            def ldg(d):
                return B.wload([win[:, 2560 + br * 1024 + d * 128:2560 + br * 1024 + (d + 1) * 128] for br in range(3)], (8, 384))

            pair = [0]
            for d in range(8):
                for j in range(d, d + 1):
                    if j not in loaded:
                        loaded[j] = ldg(j)
                Wg = loaded[d]
                for tl, tt in enumerate(tiles):
                    acc = B.scratch()
                    for br in range(3):
                        bg = (pair[0] % 3) * 2; bp = bg + 1; pair[0] += 1
                        for k in range(8):
                            m = mxh(k, tl)
                            B.mm(psv(bg), Wg.ap[:, k, br * 128:(br + 1) * 128], m.ap, k == 0, k == 7, reads=[Wg, m], writes=[PS[bg]])
                        if br == 0:
                            chs = [(Wpac.ap[:, kc, d * 128:(d + 1) * 128], yT(6 + kc, tt), Wpac) for kc in range(2)]
                        elif br == 1:
                            chs = [(Wpb.ap[:, kc, d * 128:(d + 1) * 128], yT(kc, tt), Wpb) for kc in range(4)]
                        else:
                            chs = [(Wpac.ap[:, 2 + kc, d * 128:(d + 1) * 128], yT(4 + kc, tt), Wpac) for kc in range(2)]
                        for ii, (wap, yv, wv) in enumerate(chs):
                            B.mm(psv(bp), wap, yv.ap, ii == 0, ii == len(chs) - 1, reads=[wv, yv], writes=[PS[bp]])
                        sg = B.scratch()
                        B.act(sg.ap, psv(bg), AF.Sigmoid, reads=[PS[bg]], writes=[sg])
                        if br == 0:
                            B.dve(lambda e, acc=acc, sg=sg, bp=bp: e.tensor_tensor(acc.ap, sg.ap, psv(bp), ALU.mult), reads=[sg, PS[bp]], writes=[acc])
                        else:
                            B.dve(lambda e, sg=sg, bp=bp: e.tensor_tensor(sg.ap, sg.ap, psv(bp), ALU.mult), reads=[sg, PS[bp]], writes=[sg])
                            dstv = mixed(d, tl) if br == 2 else acc
                            B.dve(lambda e, acc=acc, sg=sg, dstv=dstv: e.tensor_tensor(dstv.ap, acc.ap, sg.ap, ALU.add), reads=[acc, sg], writes=[dstv])
            P.phase = "%s.mix|wout" % pname[0]
            wo = dr["w_out"][l]
            loaded = {}
            for dp in range(4):
                for j in range(dp, dp + 1):
                    if j not in loaded:
                        loaded[j] = B.wload([wo[:, j * 256:(j + 1) * 256]], (8, 256))
                Wo = loaded[dp]
                for dd in range(2):
                    d = dp * 2 + dd
                    for tl, tt in enumerate(tiles):
                        b = nb()
                        for k in range(8):
                            m = mixed(k, tl)
                            B.mm(psv(b), Wo.ap[:, k, dd * 128:(dd + 1) * 128], m.ap, k == 0, k == 7, reads=[Wo, m], writes=[PS[b]])
                        x = xa(d, tt)
                        B.dve(lambda e, x=x, b=b, d=d: e.scalar_tensor_tensor(x.ap, psv(b), G_(l, ci, 1, d), x.ap, ALU.mult, ALU.add),
                              reads=[PS[b], x, Gv], writes=[x])
            for tt in tiles:
                ln_later(l, 1, tt, False)

    import os
    kpass = os.environ.get("KPASS", "ps")
    klayers = int(os.environ.get("KLAYERS", "2"))
    for pas, (xname, yname, NT, ci) in enumerate((("xpT", "ypT", 2, 0), ("xsT", "ysT", 4, 1))):
        if "ps"[pas] not in kpass:
            continue
        pname[0] = "PS"[pas]
        ntok = NT * 512
        xin = dr[xname].rearrange("(c p) t -> p c t", p=128)
        if pas == 0:
            toff[0] = 2
            up = xa_half(1)
            B.dma("sp", up.ap, xin, "xa_in", reads=[], writes=[up])
            if "s" in kpass:
                lo_ = xa_half(0)
                xs_in = dr["xsT"].rearrange("(c p) t -> p c t", p=128)
                B.dma("sp", lo_.ap, xs_in[:, :, 0:1024], "xa_in2", reads=[], writes=[lo_])
        else:
            toff[0] = 0
            if "p" not in kpass:
                lo_ = xa_half(0)
                B.dma("sp", lo_.ap, xin[:, :, 0:1024], "xa_in2", reads=[], writes=[lo_])
            up = xa_half(1)
            B.dma("sp", up.ap, xin[:, :, 1024:2048], "xa_in", reads=[], writes=[up])
        for tt in range(NT):
            if pas == 1 and tt >= 2:
                pend_ln[tt] = ("scale",)
            else:
                alpha_scale(tt)
        import os
        stage = int(os.environ.get("KSTAGE", "9"))
        for l in range(klayers):
            if stage >= 2:
                ffn(l, 1, NT, ci, False)
            own = (pas == 1 and l == klayers - 1 and os.environ.get("KOWN", "1") == "1")
            if stage >= 3:
                mixer(l, NT, ci, pas == 1, own)
            if stage >= 2:
                ffn(l, 2, NT, ci, l == 1, [[0]] if own else None)
        for t_ in range(4):
            ensure_ln(t_)
        yout = dr[yname].rearrange("(c p) t -> p c t", p=128)
        if pas == 0:
            up = xa_half(1)
            B.dma("sp", yout, up.ap, "xa_out", reads=[up], writes=[("out", "y")])
        else:
            lo_ = xa_half(0)
            B.dma("sp", yout, lo_.ap[:, :, 0:512], "xa_out", reads=[lo_], writes=[("out", "y")])
    fin = P.add("sp", None, reads=[("out", "y")], writes=[])
    seen = {}
    for op in P.ops["sp"]:
        if op.dma is not None and op.dma[0][0] in ("scr",) or (op.dma is not None and op.dma[0] == "xa_out"):
            seen[op.dma[0]] = op
    fin.deps = list(seen.values())


SBUF_SPECS = [
    ("XA", 16384, F32), ("G", 20480, BF16), ("H", 16384, BF16), ("WR", 16384, BF16),
    ("SCR", 5 * 512, F32), ("BS", 3 * 512, BF16), ("FR", 2048, F32), ("LNS", 1536, F32),
    ("condT", 16, F32), ("bmodT", 144, F32), ("lngT", 48, F32), ("lnbT", 48, F32), ("poolsc", 4, F32),
    ("sgT", 2, F32), ("onesf", 128, F32), ("onesb", 384, BF16),
    ("Pp", 2048, BF16), ("Ps", 512, BF16), ("poolbd", 512, BF16), ("scb", 16, BF16),
    ("MT", 288, F32), ("MA", 96, F32), ("MG", 96, F32), ("AG", 48, F32), ("AB", 48, F32), ("LM", 16, F32), ("SM", 8, F32), ("EPS", 2, F32), ("wsel", 4, F32),
]


def build_nc():
    from contextlib import ExitStack
    nc = bass.Bass("TRN2", target_bir_lowering=False)
    B = build(nc)
    with ExitStack() as es:
        sb = {}
        for name, n, dt in SBUF_SPECS:
            t = es.enter_context(nc.sbuf_tensor("sb_" + name, [128, n], dt))
            sb[name] = (t, 4 if dt == F32 else 2)
        ps = []
        for i in range(7):
            ps.append(es.enter_context(nc.psum_tensor("ps%d" % i, [128, 512], F32)))
        ps.append(es.enter_context(nc.psum_tensor("ps7", [128, 1024], BF16)))
        B.sb = sb
        B.wq_init(None)
        emit_all(B, nc, sb, ps)
        rec = (B.wq_rec, B.wq_last)
        B.P = Prog()
        B.scr_i = 0
        B.bs_i = 0
        B.wq_init(rec)
        emit_all(B, nc, sb, ps)
        P = B.P
        import os, json
        if os.environ.get("KTAGS"):
            json.dump(P.tags, open(os.environ["KTAGS"], "w"))
        esem = {e: es.enter_context(nc.semaphore("se_" + e)) for e in Prog.ENGS}
        dsem = {}
        for k in P.dcount:
            dsem[k] = es.enter_context(nc.semaphore("sd_%d" % len(dsem)))
        block = es.enter_context(nc.Block())
        engs = {}

        def mk(ename):
            def f(eng):
                engs[ename] = eng
                P_emit_one(P, ename, eng, esem, dsem)
            return f
        block.tensor(mk("pe"))
        block.scalar(mk("act"))
        block.vector(mk("dve"))
        block.gpsimd(mk("pool"))
        block.sync(mk("sp"))
    return nc


def P_emit_one(P, e, eng, esem, dsem):
    if not getattr(P, "_sig_done", False):
        for e2 in P.ENGS:
            c = 0
            for op in P.ops[e2]:
                if op.dma is None and op.signal:
                    c += 1
                    op.sigval = c
        P._sig_done = True
    waited = {}
    for op in P.ops[e]:
        need = {}
        for d in op.deps:
            if d.dma is not None:
                key = ("d", d.dma[0]); val = d.dma[1]; sem = dsem[d.dma[0]]
            else:
                key = ("e", d.eng); val = d.sigval; sem = esem[d.eng]
            if waited.get(key, 0) >= val:
                continue
            if key not in need or need[key][1] < val:
                need[key] = (sem, val)
        for key, (sem, val) in need.items():
            eng.wait_ge(sem, val)
            waited[key] = val
        if op.fn is None:
            continue
        ins = op.fn(eng)
        if op.dma is not None:
            ins.then_inc(dsem[op.dma[0]], 16)
        elif op.signal:
            ins.then_inc(esem[e], 1)


def _consts():
    def pmat(n, w):
        t = np.arange(n)
        lo = np.clip(t - w // 2, 0, n); hi = np.clip(t + w // 2, 0, n)
        s = np.arange(n)[:, None]
        M = ((s >= lo[None, :]) & (s < hi[None, :])).astype(np.float64) / (hi - lo)[None, :]
        M -= np.eye(n)
        return M.astype(np.float32)
    wins = (2, 4, 8, 16)
    Pp = np.zeros((128, 4, 2, 256), np.float32)
    Ps = np.zeros((128, 4, 128), np.float32)
    for g, w in enumerate(wins):
        M = pmat(256, w)
        for sblk in range(2):
            Pp[:, g, sblk, :] = M[sblk * 128:(sblk + 1) * 128, :]
        M64 = pmat(64, w)
        Ps[0:64, g, 0:64] = M64
        Ps[64:128, g, 64:128] = M64
    t = np.arange(2048)
    row = (t // 64).astype(np.float32); col = (t % 64).astype(np.float32)
    inv = (10000.0 ** (-np.arange(16, dtype=np.float32) / 16)).astype(np.float32)
    C = np.zeros((128, 2048), np.float32); S = np.zeros((128, 2048), np.float32)
    for p in range(128):
        dd = p % 64
        pos = row if dd < 32 else col
        j = dd % 32
        ang = (pos * inv[j % 16]).astype(np.float32)
        C[p] = np.cos(ang)
        S[p] = -np.sin(ang) if j < 16 else np.sin(ang)
    swap = np.array([(d + 16) if (d % 32) < 16 else (d - 16) for d in range(64)])
    return Pp.reshape(128, -1), Ps.reshape(128, -1), C, S, swap


def _na_bias(na_rpb):
    col = np.arange(64)
    col_start = np.clip(col - 8, 0, 48)
    ok = (col[None, :] >= col_start[:, None]) & (col[None, :] < col_start[:, None] + 16)
    dc = np.clip(col[None, :] - col[:, None], -15, 15) + 15
    out = np.full((2, 4, 4, 8, 128, 512), NEG, np.float32)
    for qt in range(4):
        lo = max(8 * qt - 4, 0); hi = min(8 * qt + 12, 32)
        for j, kr0 in enumerate(range(lo, hi, 2)):
            for a in range(2):
                kr = kr0 + a
                for i in range(8):
                    qr = 8 * qt + i
                    rs = min(max(qr - 4, 0), 24)
                    if not (rs <= kr < rs + 8):
                        continue
                    drr = kr - qr + 7
                    blk = na_rpb[:, :, drr, :][:, :, dc]
                    blk = np.where(ok[None, None], blk, np.float32(NEG))
                    out[:, :, qt, j, a * 64:(a + 1) * 64, i * 64:(i + 1) * 64] = blk.transpose(0, 1, 3, 2)
    return out


def _onesb():
    o = np.zeros((128, 384), np.float32)
    o[:, 0:128] = 1.0
    o[:, 128:192] = 1.0
    o[:, 320:384] = 1.0
    return o


_NC_CACHE = {}
_PREP_ONLY = False


def kernel(x_prompt, x_sample, cache_diff_k, cache_diff_v, cache_na_k, cache_na_v, c, c_ctx,
           w_mod, b_mod, ln_g, ln_b, ffn1_w1, ffn1_w3, ffn1_w2, ffn2_w1, ffn2_w3, ffn2_w2,
           w_in, pool_w, pool_scale, w_pa, w_pb, w_pc, lam_q1, lam_k1, lam_q2, lam_k2,
           subln_g, na_rpb, w_out):
    f = lambda a: np.ascontiguousarray(np.asarray(a, dtype=np.float32))
    x_prompt, x_sample = f(x_prompt), f(x_sample)
    Pp, Ps, C, S, swap = _consts()
    w_in = f(w_in)
    qcols = np.concatenate([256 + 64 * blk + swap for blk in range(8)])
    kcols = qcols + 512
    w_in_sw = f(w_in[:, :, np.concatenate([qcols, kcols])])
    poolbd = np.zeros((2, 2, 128, 128), np.float32)
    pw = f(pool_w)
    for l in range(2):
        for cc in range(2):
            poolbd[l, cc, 0:64, 0:64] = pw[l, 2 * cc]
            poolbd[l, cc, 64:128, 64:128] = pw[l, 2 * cc + 1]
    common = {
        "bmodT": f(f(b_mod).reshape(2, 72, 128).transpose(2, 0, 1).reshape(128, 144)),
        "lngT": f(f(ln_g).reshape(2, 3, 8, 128).transpose(3, 0, 1, 2).reshape(128, 48)),
        "lnbT": f(f(ln_b).reshape(2, 3, 8, 128).transpose(3, 0, 1, 2).reshape(128, 48)),
        "w_mod": f(w_mod), "ffn1_w1": f(ffn1_w1), "ffn1_w3": f(ffn1_w3), "ffn1_w2": f(ffn1_w2),
        "ffn2_w1": f(ffn2_w1), "ffn2_w3": f(ffn2_w3), "ffn2_w2": f(ffn2_w2),
        "w_in": w_in, "w_in_sw": w_in_sw, "w_pa": f(w_pa), "w_pb": f(w_pb), "w_pc": f(w_pc), "w_out": f(w_out),
        "poolbd": poolbd,
        "poolsc": f(f(pool_scale).reshape(2, 2, 128).transpose(2, 0, 1).reshape(128, 4)),
        "Pp": Pp, "Ps": Ps,
        "lamv": f(np.broadcast_to(np.stack([f(lam_q1), f(lam_k1), f(lam_q2), f(lam_k2)], axis=1).reshape(1, 512), (128, 512))),
        "sgT": f(f(subln_g).T),
        "ropeC": C, "ropeS": S, "nabias": _na_bias(f(na_rpb)),
        "onesb": _onesb(), "onesf": np.full((128, 128), 1.0 / 1024.0, np.float32),
    }
    cdk, cdv, cnk, cnv = f(cache_diff_k), f(cache_diff_v), f(cache_na_k), f(cache_na_v)
    cf, cctx = f(c), f(c_ctx)
    in_maps = []
    for core in range(8):
        b = core // 4
        m = dict(common)
        m["xpT"] = f(x_prompt[4 * core:4 * core + 4].reshape(1024, 1024).T)
        m["xsT"] = f(x_sample[b].T)
        cond = np.stack([cctx, cf[b]], axis=0)
        m["condT"] = f(cond.reshape(2, 8, 128).transpose(2, 1, 0).reshape(128, 16))
        m["cdkT"] = f(cdk[b].reshape(2, 512, 4, 128).transpose(0, 2, 3, 1))
        m["cdv"] = f(cdv[b])
        m["cnkT"] = f(cnk[b].reshape(2, 512, 2, 128).transpose(0, 2, 3, 1))
        m["cnv"] = f(cnv[b])
        ws = np.zeros((128, 4), np.float32); ws[:, core % 4] = 1.0
        m["wsel"] = ws
        in_maps.append(m)
    if _PREP_ONLY:
        return in_maps
    if "nc" not in _NC_CACHE:
        _NC_CACHE["nc"] = build_nc()
    nc = _NC_CACHE["nc"]
    res = run_bass_kernel_spmd(nc, in_maps, core_ids=list(range(8)))
    R = res.results
    return _assemble(R)


def _assemble(R):
    y_prompt = np.concatenate([np.asarray(R[i]["ypT"]).T.reshape(4, 256, 1024) for i in range(8)], axis=0)
    y_sample = np.stack([np.concatenate([np.asarray(R[4 * b_ + j]["ysT"]).T for j in range(4)], axis=0) for b_ in range(2)], axis=0)
    ndk = np.concatenate([np.asarray(R[i]["ndkT"]).transpose(2, 0, 1).reshape(4, 256, 2, 4, 2, 64).transpose(0, 2, 1, 3, 4, 5)
                          for i in range(8)], axis=0)
    ndv = np.concatenate([np.asarray(R[i]["ndv"]).reshape(2, 4, 256, 4, 128).transpose(1, 0, 2, 3, 4) for i in range(8)], axis=0)
    nnk = np.concatenate([np.asarray(R[i]["nnkT"]).transpose(2, 0, 1).reshape(4, 256, 2, 4, 64).transpose(0, 2, 1, 3, 4)
                          for i in range(8)], axis=0)
    nnv = np.concatenate([np.asarray(R[i]["nnv"]).reshape(2, 4, 256, 4, 64).transpose(1, 0, 2, 3, 4) for i in range(8)], axis=0)
    o = lambda a: np.ascontiguousarray(a, dtype=np.float32)
    return (o(y_prompt), o(y_sample), o(ndk), o(ndv), o(nnk), o(nnv))
```
